# Optimizing a Trainium2 kernel written in Bass

```python
import jax, jax.numpy as jnp
from jax import lax
import numpy as np

D_MODEL = 1024
BATCH = 4
SEQ = 4096
DEPTH = 4
DEC_BATCH = 32
DEC_SEQ = 16
PAST_LEN = 1024

CHUNK = 64
Q_BLOCK = 2 * CHUNK
HEAD_DIM = 64
H_R = 8
H_F = 8
W_R = H_R * HEAD_DIM
W_F = H_F * HEAD_DIM
DECAY_LORA = 64
AAA_LORA = 64
GATE_LORA = 128
R_COLS = 3 * W_R + DECAY_LORA + AAA_LORA + GATE_LORA
F_COLS = 3 * W_F + H_F
G_COLS = 2 * D_MODEL
P_TOT = R_COLS + F_COLS + G_COLS
SPLIT_R = [W_R, 2 * W_R, 3 * W_R, 3 * W_R + DECAY_LORA, 3 * W_R + DECAY_LORA + AAA_LORA]
D_FF = -(-8 * D_MODEL // (3 * 256)) * 256
EPS = 1e-6
GN_EPS = 64e-5
SCALE = HEAD_DIM ** -0.5

kernel_name = "rwkv7_fox_gated_streaming_step"


def rms_norm(x, g):
    xf = x.astype(jnp.float32)
    y = xf * lax.rsqrt(jnp.mean(xf * xf, axis=-1, keepdims=True) + EPS)
    return (y * g.astype(jnp.float32)).astype(x.dtype)


def _delta_scan(S0, r, w, k, v, a, b):
    def step(S, inp):
        r_t, w_t, k_t, v_t, a_t, b_t = inp
        sa = jnp.einsum('bhij,bhj->bhi', S, a_t)
        S = S * w_t[:, :, None, :] + sa[..., None] * b_t[:, :, None, :] + v_t[..., None] * k_t[:, :, None, :]
        return S, jnp.einsum('bhij,bhj->bhi', S, r_t)
    xs = tuple(jnp.swapaxes(t, 0, 1) for t in (r, w, k, v, a, b))
    S, ys = lax.scan(step, S0, xs)
    return S, jnp.swapaxes(ys, 0, 1)


def _rwkv7(pr, prev_row, S0, W, l):
    Bn, T, _ = pr.shape
    shifted = jnp.concatenate([prev_row.astype(pr.dtype), pr[:, :-1]], axis=1)
    u = (pr + (shifted - pr) * W['rwkv_mu'][l]).astype(jnp.float32)
    r, k, v, wl, al, gl = jnp.split(u, SPLIT_R, axis=-1)
    w_raw = W['rwkv_w0'][l] + jnp.tanh(wl) @ W['rwkv_w2'][l]
    decay = jnp.exp(-jnp.exp(-jax.nn.softplus(-w_raw) - 0.5))
    a = jax.nn.sigmoid(W['rwkv_a0'][l] + al @ W['rwkv_a2'][l])
    g = jax.nn.sigmoid(gl) @ W['rwkv_g2'][l]
    heads = lambda t: t.reshape(Bn, T, H_R, HEAD_DIM)
    kk = heads(k * W['rwkv_k_k'][l])
    kk = kk * lax.rsqrt(jnp.sum(kk * kk, axis=-1, keepdims=True) + 1e-12)
    k = heads(k * (1.0 + (a - 1.0) * W['rwkv_k_a'][l]))
    r, v, a, decay = heads(r), heads(v), heads(a), heads(decay)
    S, y = _delta_scan(S0.astype(jnp.float32), r, decay, k, v, -kk, kk * a)
    mu_y = jnp.mean(y, axis=-1, keepdims=True)
    var = jnp.mean(jnp.square(y - mu_y), axis=-1, keepdims=True)
    y = ((y - mu_y) * lax.rsqrt(var + GN_EPS)).reshape(Bn, T, W_R) * W['rwkv_lnx_g'][l] + W['rwkv_lnx_b'][l]
    bonus = jnp.sum(r * k * W['rwkv_r_k'][l], axis=-1, keepdims=True) * v
    out = (y + bonus.reshape(Bn, T, W_R)) * g
    return out.astype(pr.dtype), S


def _fox_attend(q, k, v, cq, ck, q_pos):
    s = jnp.einsum('bqhd,bkhd->bhqk', q, k, preferred_element_type=jnp.float32) * SCALE
    s = s + cq[..., :, None] - ck[..., None, :]
    mask = jnp.arange(k.shape[1])[None, :] <= q_pos[:, None]
    p = jax.nn.softmax(jnp.where(mask, s, -jnp.inf), axis=-1)
    return jnp.einsum('bhqk,bkhd->bqhd', p.astype(v.dtype), v)


def _fox_prompt(q, k, v, logf):
    Bn, S = q.shape[:2]
    nb = S // Q_BLOCK
    cT = jnp.swapaxes(lax.cumsum(logf, axis=1), 1, 2)
    qb = jnp.swapaxes(q.reshape(Bn, nb, Q_BLOCK, H_F, HEAD_DIM), 0, 1)
    cqb = jnp.transpose(cT.reshape(Bn, H_F, nb, Q_BLOCK), (2, 0, 1, 3))
    def block(args):
        q_blk, cq_blk, i = args
        return _fox_attend(q_blk, k, v, cq_blk, cT, i * Q_BLOCK + jnp.arange(Q_BLOCK))
    o = lax.map(block, (qb, cqb, jnp.arange(nb)))
    return jnp.swapaxes(o, 0, 1).reshape(Bn, S, W_F)


def _fox_sample(q, k, v, logf, k_cache, v_cache, logf_cache):
    Bn, T = q.shape[:2]
    P = k_cache.shape[1]
    k_all = jnp.concatenate([k_cache.astype(k.dtype), k], axis=1)
    v_all = jnp.concatenate([v_cache.astype(v.dtype), v], axis=1)
    c = lax.cumsum(jnp.concatenate([logf_cache.astype(jnp.float32), logf], axis=1), axis=1)
    cT = jnp.swapaxes(c, 1, 2)
    o = _fox_attend(q, k_all, v_all, cT[..., P:], cT, P + jnp.arange(T))
    return o.reshape(Bn, T, W_F)


def _trunk(x, W, past):
    Bn, T, _ = x.shape
    ks, vs, lfs, Ss, shs = [], [], [], [], []
    for l in range(DEPTH):
        h = rms_norm(x, W['norm1_g'][l])
        p = h @ W['w_in'][l]
        pr = p[..., :R_COLS]
        pf = p[..., R_COLS:R_COLS + F_COLS]
        pg = p[..., R_COLS + F_COLS:]
        if past is None:
            prev_row = jnp.zeros((Bn, 1, R_COLS), x.dtype)
            S0 = jnp.zeros((Bn, H_R, HEAD_DIM, HEAD_DIM), jnp.float32)
        else:
            S0, prev_row = past[3][l], past[4][l]
        o_a, S_new = _rwkv7(pr, prev_row, S0, W, l)
        q, k, v, fl = jnp.split(pf, [W_F, 2 * W_F, 3 * W_F], axis=-1)
        q, k, v = (t.reshape(Bn, T, H_F, HEAD_DIM) for t in (q, k, v))
        logf = jax.nn.log_sigmoid((fl + W['fox_bf'][l]).astype(jnp.float32))
        if past is None:
            o_b = _fox_prompt(q, k, v, logf)
        else:
            o_b = _fox_sample(q, k, v, logf, past[0][l], past[1][l], past[2][l])
        g_a, g_b = jnp.split(jax.nn.sigmoid(pg), 2, axis=-1)
        m = g_a * (o_a @ W['p_a'][l]) + g_b * (o_b @ W['p_b'][l])
        x = x + m @ W['w_out'][l]
        h2 = rms_norm(x, W['norm2_g'][l])
        x = x + (jax.nn.silu(h2 @ W['w_gate'][l]) * (h2 @ W['w_up'][l])) @ W['w_down'][l]
        ks.append(k)
        vs.append(v)
        lfs.append(logf.astype(x.dtype))
        Ss.append(S_new.astype(x.dtype))
        shs.append(pr[:, -1:])
    y = rms_norm(x, W['final_g'])
    return y, jnp.stack(ks), jnp.stack(vs), jnp.stack(lfs), jnp.stack(Ss), jnp.stack(shs)


def setup_inputs(seed: int = 0) -> dict:
    key = jax.random.key(seed)
    ks = iter(jax.random.split(key, 40))
    nrm = lambda shape, scale: jax.random.normal(next(ks), shape, jnp.float32) * scale
    uni = lambda shape, lo, hi: jax.random.uniform(next(ks), shape, jnp.float32, lo, hi)
    return {
        'x_prompt': nrm((BATCH, SEQ, D_MODEL), 1.0),
        'x_sample': nrm((DEC_BATCH, DEC_SEQ, D_MODEL), 1.0),
        'cache_fox_k': nrm((DEPTH, DEC_BATCH, PAST_LEN, H_F, HEAD_DIM), 1.0),
        'cache_fox_v': nrm((DEPTH, DEC_BATCH, PAST_LEN, H_F, HEAD_DIM), 1.0),
        'cache_fox_logf': jax.nn.log_sigmoid(3.0 + nrm((DEPTH, DEC_BATCH, PAST_LEN, H_F), 1.0)),
        'state_rwkv': nrm((DEPTH, DEC_BATCH, H_R, HEAD_DIM, HEAD_DIM), 0.5),
        'state_shift': nrm((DEPTH, DEC_BATCH, 1, R_COLS), 1.0),
        'norm1_g': 1.0 + nrm((DEPTH, D_MODEL), 0.1),
        'w_in': nrm((DEPTH, D_MODEL, P_TOT), D_MODEL ** -0.5),
        'rwkv_mu': uni((DEPTH, R_COLS), 0.0, 1.0),
        'rwkv_w0': uni((DEPTH, W_R), -6.5, -1.5),
        'rwkv_w2': nrm((DEPTH, DECAY_LORA, W_R), 0.1),
        'rwkv_a0': nrm((DEPTH, W_R), 0.1),
        'rwkv_a2': nrm((DEPTH, AAA_LORA, W_R), 0.5 * AAA_LORA ** -0.5),
        'rwkv_g2': nrm((DEPTH, GATE_LORA, W_R), GATE_LORA ** -0.5),
        'rwkv_k_k': 0.85 + nrm((DEPTH, W_R), 0.05),
        'rwkv_k_a': 1.0 + nrm((DEPTH, W_R), 0.05),
        'rwkv_r_k': nrm((DEPTH, H_R, HEAD_DIM), 0.1),
        'rwkv_lnx_g': 1.0 + nrm((DEPTH, W_R), 0.1),
        'rwkv_lnx_b': nrm((DEPTH, W_R), 0.01),
        'fox_bf': 3.0 + nrm((DEPTH, H_F), 0.5),
        'p_a': nrm((DEPTH, W_R, D_MODEL), W_R ** -0.5),
        'p_b': nrm((DEPTH, W_F, D_MODEL), W_F ** -0.5),
        'w_out': nrm((DEPTH, D_MODEL, D_MODEL), D_MODEL ** -0.5),
        'norm2_g': 1.0 + nrm((DEPTH, D_MODEL), 0.1),
        'w_gate': nrm((DEPTH, D_MODEL, D_FF), D_MODEL ** -0.5),
        'w_up': nrm((DEPTH, D_MODEL, D_FF), D_MODEL ** -0.5),
        'w_down': nrm((DEPTH, D_FF, D_MODEL), D_FF ** -0.5),
        'final_g': 1.0 + nrm((D_MODEL,), 0.1),
    }


def reference(x_prompt, x_sample, cache_fox_k, cache_fox_v, cache_fox_logf, state_rwkv, state_shift,
              norm1_g, w_in, rwkv_mu, rwkv_w0, rwkv_w2, rwkv_a0, rwkv_a2, rwkv_g2, rwkv_k_k, rwkv_k_a,
              rwkv_r_k, rwkv_lnx_g, rwkv_lnx_b, fox_bf, p_a, p_b, w_out, norm2_g, w_gate, w_up, w_down,
              final_g):
    W = dict(norm1_g=norm1_g, w_in=w_in, rwkv_mu=rwkv_mu, rwkv_w0=rwkv_w0, rwkv_w2=rwkv_w2,
             rwkv_a0=rwkv_a0, rwkv_a2=rwkv_a2, rwkv_g2=rwkv_g2, rwkv_k_k=rwkv_k_k, rwkv_k_a=rwkv_k_a,
             rwkv_r_k=rwkv_r_k, rwkv_lnx_g=rwkv_lnx_g, rwkv_lnx_b=rwkv_lnx_b, fox_bf=fox_bf,
             p_a=p_a, p_b=p_b, w_out=w_out, norm2_g=norm2_g, w_gate=w_gate, w_up=w_up,
             w_down=w_down, final_g=final_g)
    y_prompt, kp, vp, lfp, sp, shp = _trunk(x_prompt, W, None)
    y_sample, kd, vd, lfd, sd, shd = _trunk(
        x_sample, W, (cache_fox_k, cache_fox_v, cache_fox_logf, state_rwkv, state_shift))
    return (y_prompt, y_sample, kp, vp, lfp, sp, shp, kd, vd, lfd, sd, shd)
```

```python
import contextlib
import numpy as np
import concourse.bass as bass
import concourse.mybir as mybir
from concourse.bass_utils import run_bass_kernel_spmd

F32 = mybir.dt.float32
BF16 = mybir.dt.bfloat16
AF = mybir.ActivationFunctionType
ALU = mybir.AluOpType
AX = mybir.AxisListType

ENGS = ['pe', 'act', 'dve', 'pool', 'sp']


class Tok:
    __slots__ = ('w', 'r', 'name')

    def __init__(self, name=''):
        self.w = None
        self.r = {}
        self.name = name


class V:
    __slots__ = ('ap', 'toks')

    def __init__(self, ap, toks):
        self.ap = ap
        self.toks = toks

    def __getitem__(self, idx):
        return V(self.ap[idx], self.toks)


class Buf:
    def __init__(self, t, name):
        self.t = t
        self.name = name
        self.tok = Tok(name)

    def __getitem__(self, idx):
        return V(self.t[idx], (self.tok,))

    def v(self, ap):
        return V(ap, (self.tok,))


class Chan:
    def __init__(self, sem):
        self.sem = sem
        self.count = 0
        self.last = None


class Op:
    __slots__ = ('eng', 'fn', 'deps', 'sig', 'sigval', 'chan')


class SemPool:
    def __init__(self, nc):
        self.nc = nc
        self.stack = contextlib.ExitStack()
        self.esem = {e: self.stack.enter_context(nc.semaphore("e_" + e)) for e in ENGS}
        self.ecount = {e: 0 for e in ENGS}
        self.chans = {}

    def chan(self, name):
        if name not in self.chans:
            self.chans[name] = Chan(self.stack.enter_context(self.nc.semaphore("c_" + name)))
        return self.chans[name]


class Prog:
    def __init__(self, nc, name):
        self.nc = nc
        self.name = name
        self.ops = {e: [] for e in ENGS}
        self.stack = contextlib.ExitStack()
        self.pool = nc._sempool
        self.esem = self.pool.esem
        self.chans = []
        self.n = 0

    def chan(self, name):
        c = self.pool.chan(name)
        if c not in self.chans:
            c.last = None
            self.chans.append(c)
        return c

    def sbuf(self, name, shape, dt):
        t = self.stack.enter_context(self.nc.sbuf_tensor(self.name + name, list(shape), dt))
        return Buf(t, name)

    def psum(self, name, shape, dt=F32):
        t = self.stack.enter_context(self.nc.psum_tensor(self.name + name, list(shape), dt))
        return Buf(t, name)

    def op(self, eng, fn, r=(), w=(), chan=None):
        o = Op()
        o.eng = eng
        o.fn = fn
        o.chan = chan
        o.sig = False
        o.sigval = None
        deps = {}
        rt = []
        for x in r:
            rt.extend(x.toks)
        wt = []
        for x in w:
            wt.extend(x.toks)
        for t in rt:
            if t.w is not None:
                deps[id(t.w)] = t.w
        for t in wt:
            if t.w is not None:
                deps[id(t.w)] = t.w
            for x in t.r.values():
                deps[id(x)] = x
        if chan is not None:
            if chan.last is not None:
                deps[id(chan.last)] = chan.last
            chan.last = o
            chan.count += 16
            o.sigval = chan.count
            o.sig = True
        dl = []
        for d in deps.values():
            if d is o:
                continue
            if d.chan is None:
                if d.eng == eng and eng == 'pe':
                    continue
                d.sig = True
            dl.append(d)
        o.deps = dl
        key = ('c', id(chan)) if chan is not None else eng
        for t in rt:
            t.r[key] = o
        for t in wt:
            t.w = o
            t.r = {}
        self.ops[eng].append(o)
        self.n += 1
        return o

    def emit(self):
        nc = self.nc
        for e in ENGS:
            lst = self.ops[e]
            if lst and lst[-1].chan is None:
                lst[-1].sig = True
            c = self.pool.ecount[e]
            for o in lst:
                if o.chan is None and o.sig:
                    c += 1
                    o.sigval = c
            self.pool.ecount[e] = c
        maxsig = dict(self.pool.ecount)
        ops = self.ops
        esem = self.esem
        chans = self.chans

        def run(e, h):
            waited = {}
            for o in ops[e]:
                for d in o.deps:
                    s = d.chan.sem if d.chan is not None else esem[d.eng]
                    k = id(s)
                    if waited.get(k, 0) < d.sigval:
                        h.wait_ge(s, d.sigval)
                        waited[k] = d.sigval
                inst = o.fn(h)
                if o.chan is not None:
                    inst.then_inc(o.chan.sem, 16)
                elif o.sig:
                    inst.then_inc(esem[e], 1)
            for x in ENGS:
                if maxsig[x] > 0 and waited.get(id(esem[x]), 0) < maxsig[x]:
                    h.wait_ge(esem[x], maxsig[x])
            for c in chans:
                if c.count > 0 and waited.get(id(c.sem), 0) < c.count:
                    h.wait_ge(c.sem, c.count)

        with nc.Block() as block:
            @block.tensor
            def _(h):
                run('pe', h)

            @block.scalar
            def _(h):
                run('act', h)

            @block.vector
            def _(h):
                run('dve', h)

            @block.gpsimd
            def _(h):
                run('pool', h)

            @block.sync
            def _(h):
                run('sp', h)
        self.stack.close()

    def dma(self, q, out, in_, chan, **kw):
        return self.op(q, lambda h: h.dma_start(out=out.ap, in_=in_.ap, **kw), r=[in_], w=[out], chan=chan)

    def mm(self, out, lhsT, rhs, start=True, stop=True):
        return self.op('pe', lambda h: h.matmul(out.ap, lhsT.ap, rhs.ap, start=start, stop=stop),
                       r=[lhsT, rhs], w=[out])

    def tr(self, out, in_, ident):
        return self.op('pe', lambda h: h.transpose(out.ap, in_.ap, ident.ap), r=[in_, ident], w=[out])

    def act(self, out, in_, func, bias=None, scale=None):
        kw = {}
        r = [in_]
        if bias is not None:
            if isinstance(bias, V):
                kw['bias'] = bias.ap
                r.append(bias)
            else:
                kw['bias'] = bias
        if scale is not None:
            if isinstance(scale, V):
                kw['scale'] = scale.ap
                r.append(scale)
            else:
                kw['scale'] = scale
        return self.op('act', lambda h: h.activation(out.ap, in_.ap, func, **kw), r=r, w=[out])

    def tt(self, eng, out, a, b, op):
        return self.op(eng, lambda h: h.tensor_tensor(out.ap, a.ap, b.ap, op), r=[a, b], w=[out])

    def ts(self, eng, out, a, s1, op0, s2=None, op1=None):
        r = [a]
        a1 = s1.ap if isinstance(s1, V) else s1
        if isinstance(s1, V):
            r.append(s1)
        a2 = s2.ap if isinstance(s2, V) else s2
        if isinstance(s2, V):
            r.append(s2)
        kw = {}
        if op1 is not None:
            kw['op1'] = op1
        return self.op(eng, lambda h: h.tensor_scalar(out.ap, a.ap, a1, a2, op0, **kw), r=r, w=[out])

    def stt(self, eng, out, a, s, b, op0, op1):
        r = [a, b]
        sa = s.ap if isinstance(s, V) else s
        if isinstance(s, V):
            r.append(s)
        return self.op('dve', lambda h: h.scalar_tensor_tensor(out.ap, a.ap, sa, b.ap, op0, op1), r=r, w=[out])

    def copy(self, eng, out, in_):
        if eng == 'act':
            return self.op('act', lambda h: h.copy(out.ap, in_.ap), r=[in_], w=[out])
        return self.op(eng, lambda h: h.tensor_copy(out.ap, in_.ap), r=[in_], w=[out])

    def memset(self, eng, out, val):
        return self.op(eng, lambda h: h.memset(out.ap, val), r=[], w=[out])

    def recip(self, out, in_):
        return self.op('dve', lambda h: h.reciprocal(out.ap, in_.ap), r=[in_], w=[out])

    def rsum(self, out, in_):
        return self.op('dve', lambda h: h.reduce_sum(out.ap, in_.ap, AX.X), r=[in_], w=[out])


class Rot:
    def __init__(self, items):
        self.items = items
        self.i = 0

    def __call__(self):
        x = self.items[self.i % len(self.items)]
        self.i += 1
        return x


D = 1024
HD = 64
NH = 8
WR = 512
RC = 1792
FC = 1544
PT = 5384
DFF = 2816
NFF = 22
EPS = 1e-6
GN_EPS = 64e-5
SCALE = HD ** -0.5
LWC = -0.6065306597126334
NS = 4
LS = 16
CQ = RC
CK = RC + 512
CV = RC + 1024
CFL = RC + 1536
CGA = RC + FC
CGB = CGA + D
PV_G1, PV_MU, PV_W0, PV_A0, PV_KK, PV_KA, PV_RK, PV_G2, PV_N = 0, 8, 22, 26, 30, 34, 38, 42, 50


class Cfg:
    def __init__(self, seq=4096, depth=4, past=1024):
        self.seq = seq
        self.depth = depth
        self.past = past
        self.ta = seq + NS * LS
        self.tiles = [(i * 512, 512, 1, 512, 64) for i in range(seq // 512)] + [(seq, NS * LS, NS, LS, LS)]
        self.tilesA = [(i * 256, 256, 1, 256, 64) for i in range(seq // 256)] + [(seq, NS * LS, NS, LS, LS)]


class G:
    pass


def dram(nc, name, shape, dt, kind="Internal"):
    return Buf(nc.dram_tensor(name, list(shape), dt, kind=kind).ap(), name)


def declare(nc, cfg):
    g = G()
    L = cfg.depth
    S = cfg.seq
    TA = cfg.ta
    I = "ExternalInput"
    O = "ExternalOutput"
    g.x_prompt = dram(nc, "x_prompt", [S, D], F32, I)
    g.x_sample = dram(nc, "x_sample", [NS * LS, D], F32, I)
    g.cache_k = dram(nc, "cache_k", [L, NS, cfg.past, 512], F32, I)
    g.cache_v = dram(nc, "cache_v", [L, NS, cfg.past, 512], F32, I)
    g.cache_lf = dram(nc, "cache_lf", [L, NS, cfg.past, NH], F32, I)
    g.state_in = dram(nc, "state_in", [L, NS, NH, HD, HD], F32, I)
    g.shift_in = dram(nc, "shift_in", [L, NS, RC], F32, I)
    g.w_in = dram(nc, "w_in", [L, D, PT], F32, I)
    g.w2 = dram(nc, "w2", [L, 64, WR], F32, I)
    g.a2 = dram(nc, "a2", [L, 64, WR], F32, I)
    g.g2 = dram(nc, "g2", [L, 128, WR], F32, I)
    g.p_a = dram(nc, "p_a", [L, WR, D], F32, I)
    g.p_b = dram(nc, "p_b", [L, WR, D], F32, I)
    g.w_out = dram(nc, "w_out", [L, D, D], F32, I)
    g.w_gate = dram(nc, "w_gate", [L, D, DFF], F32, I)
    g.w_up = dram(nc, "w_up", [L, D, DFF], F32, I)
    g.w_down = dram(nc, "w_down", [L, DFF, D], F32, I)
    g.pvec = dram(nc, "pvec", [L, 128, PV_N], F32, I)
    g.prow = dram(nc, "prow", [L, 3, 512], F32, I)
    g.fing = dram(nc, "fing", [128, 8], F32, I)
    g.consts = dram(nc, "consts", [128, 9, 128], F32, I)
    g.hsel = dram(nc, "hsel", [128, 2], F32, I)
    g.y_prompt = dram(nc, "y_prompt", [S, D], F32, O)
    g.y_sample = dram(nc, "y_sample", [NS * LS, D], F32, O)
    g.kp = dram(nc, "kp", [L, S, 512], F32, O)
    g.vp = dram(nc, "vp", [L, S, 512], F32, O)
    g.lfp = dram(nc, "lfp", [L, S, NH], F32, O)
    g.sp_ = dram(nc, "sp", [L, NH, HD, HD], F32, O)
    g.shp = dram(nc, "shp", [L, RC], F32, O)
    g.kd = dram(nc, "kd", [L, NS * LS, 512], F32, O)
    g.vd = dram(nc, "vd", [L, NS * LS, 512], F32, O)
    g.lfd = dram(nc, "lfd", [L, NS * LS, NH], F32, O)
    g.sd = dram(nc, "sd", [L, NS, NH, HD, HD], F32, O)
    g.shd = dram(nc, "shd", [L, NS, RC], F32, O)
    g.xT = dram(nc, "s_xT", [D, TA], F32)
    for nm in ("rt", "at", "bt", "kt", "cum"):
        setattr(g, nm, dram(nc, "s_" + nm, [WR, TA], F32))
    g.vR = dram(nc, "s_vR", [TA, WR], F32)
    g.gR = dram(nc, "s_gR", [TA, WR], F32)
    g.bnR = dram(nc, "s_bnR", [TA, NH], F32)
    g.qT = dram(nc, "s_qT", [512, TA], BF16)
    g.kT = dram(nc, "s_kT", [512, TA], BF16)
    g.vF = dram(nc, "s_vF", [TA, 512], BF16)
    g.lf = dram(nc, "s_lf", [TA, NH], F32)
    g.gaT = dram(nc, "s_gaT", [D, TA], BF16)
    g.gbT = dram(nc, "s_gbT", [D, TA], BF16)
    g.oaT = dram(nc, "s_oaT", [512, TA], BF16)
    g.obT = dram(nc, "s_obT", [512, TA], BF16)
    return g


def load_consts(P, g, ch):
    c = P.sbuf("consts", [128, 9, 128], F32)
    P.dma('sp', c[:], g.consts[:], ch)
    return c


class WBuf:
    def __init__(self, P, name, nk, ncols, piece):
        self.b = P.sbuf(name, [128, nk, ncols], BF16)
        self.t = self.b.t
        self.piece = piece
        self.toks = [[Tok("%s_%d_%d" % (name, i, k)) for k in range(nk)] for i in range((ncols + piece - 1) // piece)]

    def __getitem__(self, idx):
        cs = idx[-1]
        k = idx[1]
        assert isinstance(cs, slice) and isinstance(k, int)
        p0 = cs.start // self.piece
        p1 = (cs.stop - 1) // self.piece
        return V(self.t[idx], tuple(self.toks[p][k] for p in range(p0, p1 + 1)))


def load_w_bf16(P, dst, src_ap, nk, ncols, ch):
    chs = [P.chan("w%d" % i) for i in range(8)]
    if isinstance(dst, WBuf):
        i = 0
        for c0 in range(0, ncols, dst.piece):
            c1 = min(ncols, c0 + dst.piece)
            for k in range(nk):
                P.dma('pool', dst[:, k, c0:c1], V(src_ap[k * 128:(k + 1) * 128, c0:c1], ()), chs[i % 8])
                i += 1
        return
    i = 0
    for k in range(nk):
        for c0 in range(0, ncols, 2048):
            c1 = min(ncols, c0 + 2048)
            P.dma('pool', dst[:, k, c0:c1], V(src_ap[k * 128:(k + 1) * 128, c0:c1], ()), chs[i % 8])
            i += 1


def rmsnorm_tile(P, x, nt, gcol, pv, cst, ps_ss, sqr, rs, hT, evr):
    for c in range(8):
        sq = sqr()
        P.act(sq[:, 0:nt], x[:, c, 0:nt], AF.Square)
        P.mm(ps_ss[:, 0:nt], cst[:, 1, :], sq[:, 0:nt], start=(c == 0), stop=(c == 7))
    P.act(rs[:, 0:nt], ps_ss[:, 0:nt], AF.Sqrt, bias=EPS, scale=1.0 / D)
    P.recip(rs[:, 0:nt], rs[:, 0:nt])
    for c in range(8):
        P.stt(evr(), hT[:, c, 0:nt], x[:, c, 0:nt], pv[:, gcol + c:gcol + c + 1], rs[:, 0:nt], ALU.mult, ALU.mult)


def phase_T0(nc, g, cfg):
    P = Prog(nc, "T0")
    ch = [P.chan("l%d" % i) for i in range(2)] + [P.chan("s%d" % i) for i in range(2)]
    cst = load_consts(P, g, P.chan("c"))
    xin = [P.sbuf("xin%d" % i, [128, D], F32) for i in range(2)]
    xo = [P.sbuf("xo%d" % i, [128, 8, 128], F32) for i in range(2)]
    ps = [P.psum("ps%d" % i, [128, 4, 128]) for i in range(4)]
    psr = Rot(ps)
    blocks = [(g.x_prompt, i * 128, 128, i * 128) for i in range(cfg.seq // 128)] + [(g.x_sample, 0, NS * LS, cfg.seq)]
    xTv = g.xT.t.rearrange("(c p) t -> p c t", p=128)
    evr = Rot(['dve', 'act'])
    for bi, (src, r0, nr, t0) in enumerate(blocks):
        xi = xin[bi % 2]
        xx = xo[bi % 2]
        P.dma('sp', xi[0:nr, :], src[r0:r0 + nr, :], ch[bi % 2])
        for hh in range(2):
            p_ = psr()
            for c in range(4):
                cc = hh * 4 + c
                P.tr(p_[:, c, 0:nr], xi[0:nr, cc * 128:(cc + 1) * 128], cst[0:nr, 0, 0:nr])
            P.copy(evr(), xx[:, hh * 4:hh * 4 + 4, 0:nr], p_[:, :, 0:nr])
        P.dma('sp', g.xT.v(xTv[:, :, t0:t0 + nr]), xx[:, :, 0:nr], ch[2 + bi % 2])
    P.emit()


def phase_A(nc, g, cfg, l):
    P = Prog(nc, "A%d" % l)
    S = cfg.seq
    cl = [P.chan("l%d" % i) for i in range(2)]
    cs = [P.chan("s%d" % i) for i in range(6)]
    csr = Rot(cs)
    cw = P.chan("w")
    cc_ = P.chan("c")
    cst = load_consts(P, g, cc_)
    pv = P.sbuf("pv", [128, PV_N], F32)
    P.dma('sp', pv[:], g.pvec[l], cc_)
    hsel = P.sbuf("hsel", [128, 2], F32)
    P.dma('sp', hsel[:], g.hsel[:], cc_)
    bfr = P.sbuf("bfr", [128, NH], F32)
    P.dma('sp', bfr[:], V(g.prow.t[l, 2:3, 0:NH].to_broadcast([128, NH]), g.prow.tok and (g.prow.tok,)), cc_)
    P.ts('dve', bfr[:], bfr[:], -1.0, ALU.mult)
    w = WBuf(P, "w", 8, PT, 512)
    load_w_bf16(P, w, g.w_in.t[l], 8, PT, cw)
    w2b = P.sbuf("w2b", [128, WR], BF16)
    P.memset('pool', w2b[64:128, :], 0.0)
    P.dma('pool', w2b[0:64, :], g.w2[l], cw)
    a2b = P.sbuf("a2b", [128, WR], BF16)
    P.memset('pool', a2b[0:64, :], 0.0)
    P.dma('pool', a2b[64:128, :], g.a2[l], cw)
    g2b = P.sbuf("g2b", [128, WR], BF16)
    P.dma('pool', g2b[:], g.g2[l], cw)

    xt = [P.sbuf("xt%d" % i, [128, 8, 256], F32) for i in range(2)]
    sqb = [P.sbuf("sq%d" % i, [128, 256], F32) for i in range(2)]
    sqr = Rot(sqb)
    rs = P.sbuf("rs", [128, 256], F32)
    hTs = [P.sbuf("hT%d" % i, [128, 8, 256], BF16) for i in range(2)]
    pr = P.sbuf("pr", [128, 14, 256], F32)
    sh = P.sbuf("sh", [128, 14, 256], F32)
    carry = P.sbuf("carry", [128, 14, NS], F32)
    u = sh
    msk = P.sbuf("msk", [128, 256], F32)
    obf = [P.sbuf("obf%d" % i, [128, 4, 256], BF16) for i in range(2)]
    obr = Rot(obf)
    tko = [P.sbuf("tko%d" % i, [128, 512], F32) for i in range(3)]
    tkr = Rot(tko)
    tkb = [P.sbuf("tkb%d" % i, [128, 512], BF16) for i in range(2)]
    tkbr = Rot(tkb)
    lfo = [P.sbuf("lfo%d" % i, [128, NH], F32) for i in range(2)]
    lfr = Rot(lfo)
    bno = [P.sbuf("bno%d" % i, [128, NH], F32) for i in range(2)]
    bnr = Rot(bno)
    nm = ["tw", "alb", "sgl", "sg", "cum", "cex", "G", "Gi", "Gx", "a", "kk", "kq", "rsq", "kp", "t1",
          "o_rt", "o_at", "o_bt", "o_kt", "rk"]
    T = {}
    for n_ in nm:
        dt = BF16 if n_ in ("tw", "alb", "sgl") else F32
        T[n_] = P.sbuf("r_" + n_, [128, 256], dt)
    T2 = {}
    for n_ in nm:
        if n_ in ("tw", "alb", "sgl"):
            continue
        T2[n_] = P.sbuf("r2_" + n_, [128, 256], F32)
    shT = P.sbuf("shT", [64, 512], F32)
    sin = P.sbuf("sin", [NS, RC], F32)
    ps = [P.psum("ps%d" % i, [128, 512]) for i in range(8)]
    psr = Rot(ps[0:6])
    ps_ss = ps[6]
    ps_m = ps[7]
    evr = Rot(['dve', 'pool'])
    ev2 = Rot(['act', 'dve'])

    xTv = g.xT.t.rearrange("(c p) t -> p c t", p=128)

    def proj(hT, pst, c0, ncol, nt):
        for k in range(8):
            P.mm(pst[0:ncol, 0:nt], w[:, k, c0:c0 + ncol], hT[:, k, 0:nt], start=(k == 0), stop=(k == 7))

    tiles = cfg.tilesA
    cur = [0]
    def stage1(ti):
        t0, nt, nseq, L, C = tiles[ti]
        x = xt[ti % 2]
        hT = hTs[ti % 2]
        sample = nseq > 1
        if ti + 1 < len(tiles):
            t0n, ntn = tiles[ti + 1][0], tiles[ti + 1][1]
            P.dma('sp', xt[(ti + 1) % 2][:, :, 0:ntn], g.xT.v(xTv[:, :, t0n:t0n + ntn]), cl[(ti + 1) % 2])
        rmsnorm_tile(P, x, nt, PV_G1, pv, cst, ps_ss, sqr, rs, hT, evr)
        yield
        for (c0, dst) in ((CQ, g.qT), (CK, g.kT)):
            ob_ = obr()
            for j in range(4):
                p_ = psr()
                proj(hT, p_, c0 + j * 128, 128, nt)
                P.copy(ev2(), ob_[:, j, 0:nt], p_[:, 0:nt])
            P.dma('sp', dst.v(dst.t.rearrange("(c p) t -> p c t", p=128)[:, :, t0:t0 + nt]), ob_[:, :, 0:nt], csr())
            yield
        for (c0, dst) in ((CGA, g.gaT), (CGB, g.gbT)):
            for hh in range(2):
                ob_ = obr()
                for j in range(4):
                    p_ = psr()
                    proj(hT, p_, c0 + (hh * 4 + j) * 128, 128, nt)
                    P.act(ob_[:, j, 0:nt], p_[:, 0:nt], AF.Sigmoid)
                P.dma('sp', dst.v(dst.t.rearrange("(c p) t -> p c t", p=128)[:, hh * 4:hh * 4 + 4, t0:t0 + nt]),
                      ob_[:, :, 0:nt], csr())
                yield
        for tb in range(0, nt, 128):
            nb = min(128, nt - tb)
            kout = (g.kd if sample else g.kp)
            vout = (g.vd if sample else g.vp)
            lout = (g.lfd if sample else g.lfp)
            ro = (0 if sample else t0) + tb
            for (c0, dst, isv) in ((CK, kout, False), (CV, vout, True)):
                p_ = psr()
                for k in range(8):
                    P.mm(p_[0:nb, :], hT[:, k, tb:tb + nb], w[:, k, c0:c0 + 512], start=(k == 0), stop=(k == 7))
                o_ = tkr()
                P.copy(ev2(), o_[0:nb, :], p_[0:nb, :])
                P.dma('sp', dst[l, ro:ro + nb, :], o_[0:nb, :], csr())
                if isv:
                    ob_ = tkbr()
                    P.copy('pool', ob_[0:nb, :], o_[0:nb, :])
                    P.dma('sp', g.vF[t0 + tb:t0 + tb + nb, :], ob_[0:nb, :], csr())
            p_ = psr()
            for k in range(8):
                P.mm(p_[0:nb, 0:NH], hT[:, k, tb:tb + nb], w[:, k, CFL:CFL + NH], start=(k == 0), stop=(k == 7))
            lf_ = lfr()
            P.stt('dve', lf_[0:nb, :], p_[0:nb, 0:NH], -1.0, bfr[0:nb, :], ALU.mult, ALU.add)
            P.act(lf_[0:nb, :], lf_[0:nb, :], AF.Exp)
            P.act(lf_[0:nb, :], lf_[0:nb, :], AF.Ln, bias=1.0)
            P.ts('dve', lf_[0:nb, :], lf_[0:nb, :], -1.0, ALU.mult)
            P.dma('sp', lout[l, ro:ro + nb, :], lf_[0:nb, :], csr())
            P.dma('sp', g.lf[t0 + tb:t0 + tb + nb, :], lf_[0:nb, :], csr())
            yield

    def stage2(ti):
        t0, nt, nseq, L, C = tiles[ti]
        x = xt[ti % 2]
        hT = hTs[ti % 2]
        sample = nseq > 1
        for j in range(14):
            p_ = psr()
            proj(hT, p_, j * 128, 128, nt)
            P.copy(ev2(), pr[:, j, 0:nt], p_[:, 0:nt])
            if j % 4 == 3:
                yield
        if sample:
            P.dma('sp', sin[:], g.shift_in[l], cc_)
            for j in range(14):
                P.tr(ps_m[:, j * NS:(j + 1) * NS], sin[0:NS, j * 128:(j + 1) * 128], cst[0:NS, 0, 0:NS])
            P.copy('dve', carry[:], ps_m.v(ps_m.t[:, 0:14 * NS].rearrange("p (c s) -> p c s", s=NS)))
        elif ti == 0:
            P.memset('pool', carry[:], 0.0)
        pr4 = pr.t[:, :, 0:nt].rearrange("p c (s l) -> p c s l", s=nseq)
        sh4 = sh.t[:, :, 0:nt].rearrange("p c (s l) -> p c s l", s=nseq)
        P.copy('pool', sh.v(sh4[:, :, :, 1:L]), pr.v(pr4[:, :, :, 0:L - 1]))
        P.copy('pool', sh.v(sh4[:, :, :, 0]), carry[:, :, 0:nseq])
        last_tile_of_seq = sample or (ti == len(tiles) - 2)
        if not sample:
            P.copy('pool', carry[:, :, 0:1], pr[:, :, nt - 1:nt])
        if last_tile_of_seq:
            for s_ in range(nseq):
                P.tr(ps_m[0:14, s_ * 128:(s_ + 1) * 128], pr.v(pr4[:, :, s_, L - 1]), cst[:, 0, :])
            P.copy('dve', shT.v(shT.t[0:14, 0:nseq * 128]), ps_m[0:14, 0:nseq * 128])
            if sample:
                dst = g.shd.v(g.shd.t[l].rearrange("s (c p) -> c s p", p=128))
                P.dma('sp', dst, shT.v(shT.t[0:14, 0:nseq * 128].rearrange("c (s p) -> c s p", s=nseq)), csr())
            else:
                dst = g.shp.v(g.shp.t[l].rearrange("(c p) -> c p", p=128))
                P.dma('sp', dst, shT[0:14, 0:128], csr())
        yield
        P.tt('dve', sh[:, :, 0:nt], sh[:, :, 0:nt], pr[:, :, 0:nt], ALU.subtract)
        for j in range(14):
            P.stt(evr(), u[:, j, 0:nt], sh[:, j, 0:nt], pv[:, PV_MU + j:PV_MU + j + 1], pr[:, j, 0:nt],
                  ALU.mult, ALU.add)
        if ti == 0 or sample:
            P.memset('pool', msk[:], 1.0)
            mv = msk.t[:, 0:nt].rearrange("p (c l) -> p c l", l=C)
            P.memset('pool', msk.v(mv[:, :, 0:1]), 0.0)
        yield
        P.act(T["tw"][0:64, 0:nt], u[0:64, 12, 0:nt], AF.Tanh)
        P.copy('pool', T["tw"][64:128, 0:nt], u[64:128, 12, 0:nt])
        P.act(T["sgl"][:, 0:nt], u[:, 13, 0:nt], AF.Sigmoid)
        def chain(j, TT):
            cs_ = slice(j * 128, (j + 1) * 128)
            r_ = u[:, j, 0:nt]
            k_ = u[:, 4 + j, 0:nt]
            p_ = psr()
            P.mm(p_[:, 0:nt], w2b[:, cs_], T["tw"][:, 0:nt])
            P.act(TT["sg"][:, 0:nt], p_[:, 0:nt], AF.Sigmoid, bias=pv[:, PV_W0 + j:PV_W0 + j + 1])
            P.op('dve', lambda h, o=TT["cum"].t[:, 0:nt], m=msk.t[:, 0:nt], s=TT["sg"].t[:, 0:nt]:
                 h.tensor_tensor_scan(o, m, s, 0.0, ALU.mult, ALU.add),
                 r=[msk[:], TT["sg"][:]], w=[TT["cum"][:]])
            P.tt('pool', TT["cex"][:, 0:nt], TT["cum"][:, 0:nt], TT["sg"][:, 0:nt], ALU.subtract)
            P.act(TT["G"][:, 0:nt], TT["cum"][:, 0:nt], AF.Exp, scale=LWC)
            P.act(TT["Gi"][:, 0:nt], TT["cum"][:, 0:nt], AF.Exp, scale=-LWC)
            P.act(TT["Gx"][:, 0:nt], TT["cex"][:, 0:nt], AF.Exp, scale=LWC)
            yield
            p2 = psr()
            P.mm(p2[:, 0:nt], a2b[:, cs_], T["tw"][:, 0:nt])
            P.act(TT["a"][:, 0:nt], p2[:, 0:nt], AF.Sigmoid, bias=pv[:, PV_A0 + j:PV_A0 + j + 1])
            P.ts('pool', TT["kk"][:, 0:nt], k_, pv[:, PV_KK + j:PV_KK + j + 1], ALU.mult)
            P.tt('pool', TT["kq"][:, 0:nt], TT["kk"][:, 0:nt], TT["kk"][:, 0:nt], ALU.mult)
            p3 = psr()
            P.mm(p3[:, 0:nt], cst[:, 2, :], TT["kq"][:, 0:nt])
            P.act(TT["rsq"][:, 0:nt], p3[:, 0:nt], AF.Sqrt, bias=1e-12, scale=1.0)
            P.recip(TT["rsq"][:, 0:nt], TT["rsq"][:, 0:nt])
            P.tt('dve', TT["kk"][:, 0:nt], TT["kk"][:, 0:nt], TT["rsq"][:, 0:nt], ALU.mult)
            P.ts('pool', TT["t1"][:, 0:nt], TT["a"][:, 0:nt], -1.0, ALU.add, pv[:, PV_KA + j:PV_KA + j + 1], ALU.mult)
            P.stt('dve', TT["kp"][:, 0:nt], TT["t1"][:, 0:nt], 1.0, k_, ALU.add, ALU.mult)
            yield
            P.tt('pool', TT["o_rt"][:, 0:nt], r_, TT["G"][:, 0:nt], ALU.mult)
            P.stt('dve', TT["o_at"][:, 0:nt], TT["kk"][:, 0:nt], -1.0, TT["Gx"][:, 0:nt], ALU.mult, ALU.mult)
            P.tt('pool', TT["o_bt"][:, 0:nt], TT["kk"][:, 0:nt], TT["a"][:, 0:nt], ALU.mult)
            P.tt('dve', TT["o_bt"][:, 0:nt], TT["o_bt"][:, 0:nt], TT["Gi"][:, 0:nt], ALU.mult)
            P.tt('pool', TT["o_kt"][:, 0:nt], TT["kp"][:, 0:nt], TT["Gi"][:, 0:nt], ALU.mult)
            P.stt('dve', TT["rk"][:, 0:nt], r_, pv[:, PV_RK + j:PV_RK + j + 1], TT["kp"][:, 0:nt], ALU.mult, ALU.mult)
            for nm_, dst in (("o_rt", g.rt), ("o_at", g.at), ("o_bt", g.bt), ("o_kt", g.kt), ("cum", g.cum)):
                P.dma("sp", dst[cs_, t0:t0 + nt], TT[nm_][:, 0:nt], csr())
            yield
            for tb in range(0, nt, 128):
                nb = min(128, nt - tb)
                P.mm(ps_m[0:nb, (tb // 128) * 8 + 2 * j:(tb // 128) * 8 + 2 * j + 2], TT["rk"][:, tb:tb + nb], hsel[:, :])
        for jp in (0, 2):
            gens = [chain(jp, T), chain(jp + 1, T2)]
            while gens:
                for gn in list(gens):
                    try:
                        next(gn)
                    except StopIteration:
                        gens.remove(gn)
                yield
        for tb in range(0, nt, 128):
            nb = min(128, nt - tb)
            bo = bnr()
            P.copy('dve', bo[0:nb, :], ps_m[0:nb, (tb // 128) * 8:(tb // 128) * 8 + 8])
            P.dma('sp', g.bnR[t0 + tb:t0 + tb + nb, :], bo[0:nb, :], csr())
            p_ = psr()
            for j in range(4):
                P.tr(p_[0:nb, j * 128:(j + 1) * 128], u[:, 8 + j, tb:tb + nb], cst[:, 0, :])
            o_ = tkr()
            P.copy(ev2(), o_[0:nb, :], p_[0:nb, :])
            P.dma('sp', g.vR[t0 + tb:t0 + tb + nb, :], o_[0:nb, :], csr())
            yield
            p_ = psr()
            P.mm(p_[0:nb, :], T["sgl"][:, tb:tb + nb], g2b[:, :])
            o_ = tkr()
            P.copy(ev2(), o_[0:nb, :], p_[0:nb, :])
            P.dma('sp', g.gR[t0 + tb:t0 + tb + nb, :], o_[0:nb, :], csr())

    def drive(gens):
        gens = list(gens)
        while gens:
            for gn in list(gens):
                try:
                    next(gn)
                except StopIteration:
                    gens.remove(gn)

    P.dma('sp', xt[0][:, :, 0:tiles[0][1]], g.xT.v(xTv[:, :, 0:tiles[0][1]]), cl[0])
    drive([stage1(0)])
    for ti in range(len(tiles)):
        gl = [stage2(ti)]
        if ti + 1 < len(tiles):
            gl.append(stage1(ti + 1))
        drive(gl)
    P.emit()


def phase_BR(nc, g, cfg, l):
    P = Prog(nc, "BR%d" % l)
    S = cfg.seq
    cc_ = P.chan("c")
    cl = [P.chan("l%d" % i) for i in range(2)]
    cs = [P.chan("s%d" % i) for i in range(3)]
    csr = Rot(cs)
    cst = load_consts(P, g, cc_)
    mk = {}
    for nm_, ci in (("I", 0), ("su", 5), ("sl", 6), ("iu", 3)):
        mk[nm_] = P.sbuf("mk_" + nm_, [64, NH, 64], F32)
        for h_ in range(NH):
            P.copy('pool', mk[nm_][:, h_, :], cst[0:64, ci, 0:64])
    rows = P.sbuf("rows", [64, 2, 512], F32)
    P.dma('sp', rows[:], V(g.prow.t[l:l + 1, 0:2, :].to_broadcast([64, 2, 512]), (g.prow.tok,)), cc_)

    GT = 128
    NG = 2
    streams = ("rt", "at", "bt", "kt", "cum")
    NSB = 3
    cl = [P.chan("l%d" % i) for i in range(NSB)]
    sb = [{n_: P.sbuf("%s%d" % (n_, i), [64, NH, GT], F32) for n_ in streams} for i in range(NSB)]
    vb = [P.sbuf("v%d" % i, [64, NG, 512], F32) for i in range(NSB)]
    gb = [P.sbuf("g%d" % i, [64, NG, 512], F32) for i in range(NSB)]
    bb = [P.sbuf("bn%d" % i, [64, NG, NH], F32) for i in range(NSB)]
    for i in range(NSB):
        for n_ in ("rt", "at", "bt", "kt"):
            sb[i][n_ + "b"] = P.sbuf("%sb%d" % (n_, i), [64, NH, GT], BF16)
    vbb = [P.sbuf("vb16_%d" % i, [64, NG, 512], BF16) for i in range(NSB)]
    MN = ("X", "X2", "AkT", "ArbT", "ArkT", "btok", "ktok")
    M = [{n_: [P.sbuf("m_%s%d_%d" % (n_, par, i), [64, NH, 64], F32 if n_ in ("X", "X2") else BF16)
               for i in range(NG)] for n_ in MN} for par in range(2)]
    Mtmp = {n_: [P.sbuf("m_%s_%d" % (n_, i), [64, NH, 64], F32) for i in range(NG)] for n_ in ("N", "A", "N2", "A2")}
    for par in range(2):
        M[par].update(Mtmp)
    GC = [[P.sbuf("gc%d_%d" % (par, i), [64, NH], F32) for i in range(NG)] for par in range(2)]
    ST = [P.sbuf("st%d" % i, [64, NH, 64], F32) for i in range(2)]
    Wsb = P.sbuf("wsb", [64, NH, 64], BF16)
    Xb = [[P.sbuf("xb%d_%d" % (par, i), [64, NH, 64], BF16) for i in range(NG)] for par in range(2)]
    Usb = [P.sbuf("usb%d" % i, [64, NH, 64], BF16) for i in range(2)]
    yt = [{n_: P.sbuf("y_%s%d" % (n_, i), [64, NH, 64], F32) for n_ in ("y", "o")} for i in range(2)]
    st1 = [{n_: P.sbuf("s_%s%d" % (n_, i), [64, NH], F32) for n_ in ("s1", "s2", "mu", "var", "rstd")} for i in range(2)]
    oT = [P.sbuf("oT%d" % i, [128, 4, 64], BF16) for i in range(2)]
    sio = P.sbuf("sio", [64, NH, 64], F32)
    ps = [P.psum("ps%d" % i, [128, 512]) for i in range(8)]
    psr = Rot(ps[0:5])
    psq = Rot(ps[5:8])
    evr = Rot(['act', 'dve'])

    def ps3(p_, C1, C2):
        return p_.v(p_.t[0:C1, :].rearrange("p (h c) -> p h c", h=NH)[:, :, 0:C2])

    def psc(p_, C):
        return p_.v(p_.t[0:C, 0:NH * C].rearrange("p (h c) -> p h c", h=NH))

    groups = [(i * GT, 64, 2, 'p', 0) for i in range(S // GT)] + [(S, LS, 2, 's', 0), (S + 2 * LS, LS, 2, 's', 2)]
    nprompt_groups = S // GT
    state = {"stcur": 0, "ycnt": 0}
    P.memset('pool', ST[0][:], 0.0)
    finalX = {}

    def load_group(gi):
        t0, C, nch, kind, sq0 = groups[gi]
        b = gi % NSB
        n = C * nch
        for n_ in streams:
            src = getattr(g, n_)
            P.dma('sp', sb[b][n_][:, :, 0:n], src.v(src.t.rearrange("(h j) t -> j h t", j=64)[:, :, t0:t0 + n]), cl[b])
        P.dma('sp', vb[b][0:C, 0:nch, :], g.vR.v(g.vR.t[t0:t0 + n, :].rearrange("(g t) c -> t g c", t=C)), cl[b])
        P.dma('sp', gb[b][0:C, 0:nch, :], g.gR.v(g.gR.t[t0:t0 + n, :].rearrange("(g t) c -> t g c", t=C)), cl[b])
        P.dma('sp', bb[b][0:C, 0:nch, :], g.bnR.v(g.bnR.t[t0:t0 + n, :].rearrange("(g t) c -> t g c", t=C)), cl[b])

    def indep(gi):
        t0, C, nch, kind, sq0 = groups[gi]
        b = gi % NSB
        s_ = sb[b]
        Mg = M[gi % 2]
        GCg = GC[gi % 2]
        nsteps = {64: 5, 16: 3}[C]

        def cs_(n_, ci, h_):
            return s_[n_][:, h_, ci * C:(ci + 1) * C]

        def stage(outname, lname, rname, mask):
            for ci in range(nch):
                p_ = psr()
                for h_ in range(NH):
                    P.mm(p_[0:C, h_ * C:(h_ + 1) * C], cs_(lname, ci, h_), cs_(rname, ci, h_))
                P.tt('dve', Mg[outname][ci][0:C, :, 0:C], psc(p_, C), mk[mask][0:C, :, 0:C], ALU.mult)

        n_ = C * nch
        for nm_ in ("rt", "at", "bt", "kt"):
            P.copy('pool', s_[nm_ + "b"][:, :, 0:n_], s_[nm_][:, :, 0:n_])
        P.copy('pool', vbb[b][0:C, 0:nch, :], vb[b][0:C, 0:nch, :])
        stage("N", "bt", "at", "su")
        yield
        stage("A", "at", "bt", "sl")
        yield
        for ci in range(nch):
            P.tt('pool', Mg["X"][ci][0:C, :, 0:C], Mg["N"][ci][0:C, :, 0:C], mk["I"][0:C, :, 0:C], ALU.add)
            P.act(GCg[ci][:, :], s_["cum"][:, :, (ci + 1) * C - 1], AF.Exp, scale=LWC)
        nN, nA, nN2, nA2, nX, nX2 = "N", "A", "N2", "A2", "X", "X2"
        extra = [("AkT", "ktb", "atb", "su"), ("ArbT", "btb", "rtb", "iu"), ("ArkT", "ktb", "rtb", "iu")]
        trs = [("btok", "bt"), ("ktok", "kt")]
        for it in range(nsteps):
            last = it == nsteps - 1
            for ci in range(nch):
                if not last:
                    p_ = psr()
                    for h_ in range(NH):
                        P.mm(p_[0:C, h_ * C:(h_ + 1) * C], Mg[nA][ci][0:C, h_, 0:C], Mg[nN][ci][0:C, h_, 0:C])
                    P.copy('act', Mg[nN2][ci][0:C, :, 0:C], psc(p_, C))
                p_ = psr()
                for h_ in range(NH):
                    P.mm(p_[0:C, h_ * C:(h_ + 1) * C], Mg[nN][ci][0:C, h_, 0:C], Mg[nA][ci][0:C, h_, 0:C])
                P.copy('act', Mg[nA2][ci][0:C, :, 0:C], psc(p_, C))
            if extra:
                stage(*extra.pop(0))
            elif trs:
                oname, sname = trs.pop(0)
                for ci in range(nch):
                    p_ = psr()
                    for h_ in range(NH):
                        P.tr(p_[0:C, h_ * 64:(h_ + 1) * 64], cs_(sname, ci, h_), cst[0:64, 0, 0:64])
                    P.copy('act', Mg[oname][ci][0:C, :, :], ps3(p_, C, 64))
            yield
            for ci in range(nch):
                p_ = psr()
                for h_ in range(NH):
                    P.mm(p_[0:C, h_ * C:(h_ + 1) * C], Mg[nA2][ci][0:C, h_, 0:C], Mg[nX][ci][0:C, h_, 0:C])
                P.tt('dve', Mg[nX2][ci][0:C, :, 0:C], psc(p_, C), Mg[nX][ci][0:C, :, 0:C], ALU.add)
            yield
            nN, nN2 = nN2, nN
            nA, nA2 = nA2, nA
            nX, nX2 = nX2, nX
        while extra:
            stage(*extra.pop(0))
            yield
        while trs:
            oname, sname = trs.pop(0)
            for ci in range(nch):
                p_ = psr()
                for h_ in range(NH):
                    P.tr(p_[0:C, h_ * 64:(h_ + 1) * 64], cs_(sname, ci, h_), cst[0:64, 0, 0:64])
                P.copy('act', Mg[oname][ci][0:C, :, :], ps3(p_, C, 64))
            yield
        finalX[gi] = nX
        for ci in range(nch):
            P.copy('pool', Xb[gi % 2][ci][0:C, :, 0:C], Mg[nX][ci][0:C, :, 0:C])

    def seq(gi):
        t0, C, nch, kind, sq0 = groups[gi]
        b = gi % NSB
        s_ = sb[b]
        Mg = M[gi % 2]
        GCg = GC[gi % 2]

        def cs_(n_, ci, h_):
            return s_[n_][:, h_, ci * C:(ci + 1) * C]

        for ci in range(nch):
            tc0 = t0 + ci * C
            if kind == 's':
                P.dma('sp', sio[:], V(g.state_in.t[l, sq0 + ci].rearrange("h i j -> i h j"), (g.state_in.tok,)), cc_)
                p_ = psq()
                for h_ in range(NH):
                    P.tr(p_[0:64, h_ * 64:(h_ + 1) * 64], sio[:, h_, :], cst[0:64, 0, 0:64])
                state["stcur"] = 0
                P.copy('dve', ST[0][:], ps3(p_, 64, 64))
            So = ST[state["stcur"]]
            Sn = ST[1 - state["stcur"]]
            X = Xb[gi % 2][ci]
            Vc = vb[b]
            Vm = vbb[b]
            p_ = psq()
            for h_ in range(NH):
                P.mm(p_[0:C, h_ * 64:(h_ + 1) * 64], cs_("at", ci, h_), So[:, h_, :], start=True, stop=False)
                P.mm(p_[0:C, h_ * 64:(h_ + 1) * 64], Mg["AkT"][ci][0:C, h_, 0:C], Vm[0:C, ci, h_ * 64:(h_ + 1) * 64],
                     start=False, stop=True)
            P.copy('act', Wsb[0:C, :, :], ps3(p_, C, 64))
            yield
            p_ = psq()
            for h_ in range(NH):
                P.mm(p_[0:C, h_ * 64:(h_ + 1) * 64], X[0:C, h_, 0:C], Wsb[0:C, h_, :])
            U = Usb[state["ycnt"] % 2]
            P.copy('dve', U[0:C, :, :], ps3(p_, C, 64))
            yield
            p_ = psq()
            for h_ in range(NH):
                o_ = p_[0:64, h_ * 64:(h_ + 1) * 64]
                P.mm(o_, cst[0:64, 0, 0:64], So[:, h_, :], start=True, stop=False)
                P.mm(o_, Mg["btok"][ci][0:C, h_, :], U[0:C, h_, :], start=False, stop=False)
                P.mm(o_, Mg["ktok"][ci][0:C, h_, :], Vm[0:C, ci, h_ * 64:(h_ + 1) * 64], start=False, stop=True)
            P.tt('dve', Sn[:], ps3(p_, 64, 64), GCg[ci].v(GCg[ci].t[:, :, None].to_broadcast([64, NH, 64])), ALU.mult)
            py = psq()
            for h_ in range(NH):
                o_ = py[0:C, h_ * 64:(h_ + 1) * 64]
                P.mm(o_, cs_("rt", ci, h_), So[:, h_, :], start=True, stop=False)
                P.mm(o_, Mg["ArbT"][ci][0:C, h_, 0:C], U[0:C, h_, :], start=False, stop=False)
                P.mm(o_, Mg["ArkT"][ci][0:C, h_, 0:C], Vm[0:C, ci, h_ * 64:(h_ + 1) * 64], start=False, stop=True)
            state["stcur"] = 1 - state["stcur"]
            yb = yt[state["ycnt"] % 2]
            sb_ = st1[state["ycnt"] % 2]
            state["ycnt"] += 1
            y = yb["y"]
            P.copy('act', y[0:C], ps3(py, C, 64))
            yield
            P.rsum(sb_["s1"][0:C, :], y[0:C])
            P.tt('pool', yb["o"][0:C], y[0:C], y[0:C], ALU.mult)
            P.rsum(sb_["s2"][0:C, :], yb["o"][0:C])
            P.ts('dve', sb_["mu"][0:C, :], sb_["s1"][0:C, :], 1.0 / 64, ALU.mult)
            P.tt('dve', sb_["var"][0:C, :], sb_["mu"][0:C, :], sb_["mu"][0:C, :], ALU.mult)
            P.stt('dve', sb_["var"][0:C, :], sb_["s2"][0:C, :], 1.0 / 64, sb_["var"][0:C, :], ALU.mult, ALU.subtract)
            P.act(sb_["rstd"][0:C, :], sb_["var"][0:C, :], AF.Sqrt, bias=GN_EPS, scale=1.0)
            P.recip(sb_["rstd"][0:C, :], sb_["rstd"][0:C, :])

            def bc(bf_):
                return bf_.v(bf_.t[0:C, :, None].to_broadcast([C, NH, 64]))
            P.tt('pool', y[0:C], y[0:C], bc(sb_["mu"]), ALU.subtract)
            P.tt('pool', y[0:C], y[0:C], bc(sb_["rstd"]), ALU.mult)
            yield
            r3 = rows.t[0:C].rearrange("p a (h c) -> p a h c", h=NH)
            P.tt('pool', y[0:C], y[0:C], rows.v(r3[:, 0]), ALU.mult)
            P.tt('pool', y[0:C], y[0:C], rows.v(r3[:, 1]), ALU.add)
            v3 = Vc.v(Vc.t[0:C, ci, :].rearrange("p (h c) -> p h c", h=NH))
            g3 = gb[b].v(gb[b].t[0:C, ci, :].rearrange("p (h c) -> p h c", h=NH))
            bn3 = bb[b].v(bb[b].t[0:C, ci, :, None].to_broadcast([C, NH, 64]))
            P.tt('pool', yb["o"][0:C], v3, bn3, ALU.mult)
            P.tt('pool', y[0:C], y[0:C], yb["o"][0:C], ALU.add)
            P.tt('pool', yb["o"][0:C], y[0:C], g3, ALU.mult)
            p_ = psq()
            ov = yb["o"].t[0:C].rearrange("p h c -> p (h c)")
            for j in range(4):
                P.tr(p_[:, j * 64:j * 64 + C], yb["o"].v(ov[:, j * 128:(j + 1) * 128]), cst[0:C, 0, 0:C])
            ot = oT[state["ycnt"] % 2]
            P.copy('act', ot[:, :, 0:C], p_.v(p_.t[:, 0:256].rearrange("p (j c) -> p j c", j=4)[:, :, 0:C]))
            P.dma('sp', g.oaT.v(g.oaT.t.rearrange("(c p) t -> p c t", p=128)[:, :, tc0:tc0 + C]), ot[:, :, 0:C], csr())
            if kind == 's' or (gi == nprompt_groups - 1 and ci == nch - 1):
                Sf = ST[state["stcur"]]
                p_ = psq()
                for h_ in range(NH):
                    P.tr(p_[0:64, h_ * 64:(h_ + 1) * 64], Sf[:, h_, :], cst[0:64, 0, 0:64])
                P.copy('dve', sio[:], ps3(p_, 64, 64))
                dst = g.sd.t[l, sq0 + ci] if kind == 's' else g.sp_.t[l]
                dtok = g.sd if kind == 's' else g.sp_
                P.dma('sp', dtok.v(dst.rearrange("h i j -> i h j")), sio[:], cc_)
            yield

    def drive(gens):
        gens = list(gens)
        while gens:
            for gn in list(gens):
                try:
                    next(gn)
                except StopIteration:
                    gens.remove(gn)

    load_group(0)
    if len(groups) > 1:
        load_group(1)
    drive([indep(0)])
    for gi in range(len(groups)):
        if gi + 2 < len(groups):
            load_group(gi + 2)
        gl = [seq(gi)]
        if gi + 1 < len(groups):
            gl.append(indep(gi + 1))
        drive(gl)
    P.emit()


def phase_BF(nc, g, cfg, l):
    P = Prog(nc, "BF%d" % l)
    S = cfg.seq
    PAST = cfg.past
    NKB = S // 128
    NPB = PAST // 128
    cc_ = P.chan("c")
    cl = [P.chan("l%d" % i) for i in range(3)]
    cs = [P.chan("s%d" % i) for i in range(2)]
    csr = Rot(cs)
    cst = load_consts(P, g, cc_)
    onesb = P.sbuf("onesb", [128, 128], BF16)
    P.memset('pool', onesb[:], 1.0)
    qz = {0: [P.sbuf("qzE%d" % i, [128, 512], BF16) for i in range(2)],
          1: [P.sbuf("qzO%d" % i, [128, 512], BF16) for i in range(2)]}
    for par in (0, 1):
        for b_ in qz[par]:
            P.memset('pool', b_[:], 0.0)
    qzc = [0]
    trib = P.sbuf("trib", [128, 128], BF16)
    P.copy('pool', trib[:], cst[:, 3, :])
    qT = P.sbuf("qT", [128, 4, S], BF16)
    kT = P.sbuf("kT", [128, 4, S], BF16)
    vF = P.sbuf("vF", [128, NKB, 512], BF16)
    P.dma('sp', qT[:], g.qT.v(g.qT.t.rearrange("(c p) t -> p c t", p=128)[:, :, 0:S]), cl[0])
    P.dma('sp', kT[:], g.kT.v(g.kT.t.rearrange("(c p) t -> p c t", p=128)[:, :, 0:S]), cl[1])
    P.dma('sp', vF[:], g.vF.v(g.vF.t[0:S, :].rearrange("(b p) c -> p b c", p=128)), cl[2])
    lf = P.sbuf("lf", [128, NKB, NH], F32)
    P.dma('sp', lf[:], g.lf.v(g.lf.t[0:S, :].rearrange("(b p) c -> p b c", p=128)), cc_)
    cum = P.sbuf("cum", [128, NKB, NH], F32)
    negc = P.sbuf("negc", [128, NKB, NH], F32)
    Rb = P.sbuf("Rb", [128, NH], F32)
    biasq = [P.sbuf("biasq%d" % i, [128, NKB, NH], F32) for i in range(2)]
    biass = P.sbuf("biass", [128, NPB + 1, NH], F32)
    pT = [P.sbuf("pT%d" % i, [128, 512], BF16) for i in range(6)]
    pTr = Rot(pT)
    rec = [P.sbuf("rec%d" % i, [128, 512], F32) for i in range(2)]
    ob = [P.sbuf("ob%d" % i, [128, 512], BF16) for i in range(2)]
    ps = [P.psum("ps%d" % i, [128, 512]) for i in range(8)]
    pss = Rot(ps[0:3])
    pso = Rot(ps[3:5])
    psd = Rot(ps[5:7])
    psm = ps[7]

    def cumsum_blocks(lfv, nblk, nlast, cumv, carry_view):
        for b_ in range(nblk):
            n_ = 128 if b_ < nblk - 1 else nlast
            first = (b_ == 0 and carry_view is None)
            P.mm(psm[0:n_, 0:NH], cst[0:n_, 3, 0:n_], lfv(b_, n_), start=True, stop=first)
            if not first:
                prev = carry_view if b_ == 0 else cumv(b_ - 1, 128)
                P.mm(psm[0:n_, 0:NH], cst[:, 4, 0:n_], prev, start=False, stop=True)
            P.copy('dve', cumv(b_, n_), psm[0:n_, 0:NH])

    cumsum_blocks(lambda b_, n_: lf[0:n_, b_, :], NKB, 128, lambda b_, n_: cum[0:n_, b_, :], None)
    P.ts('dve', negc[:], cum[:], -1.0, ALU.mult)
    obv = g.obT.t.rearrange("(c p) t -> p c t", p=128)
    sel = P.sbuf("sel", [128, NH, 128], BF16)
    P.memset('pool', sel[:], 0.0)
    P.copy('pool', sel[0:NH], cst.v(cst.t[0:NH, 0, 0:NH, None].to_broadcast([NH, NH, 128])))
    cumT = P.sbuf("cumT", [NH, S], F32)
    cqs = P.sbuf("cqs", [128, S], BF16)
    P.memset('pool', cqs[:], 0.0)
    for k4 in range(0, NKB, 4):
        for kb in range(k4, min(NKB, k4 + 4)):
            P.tr(psm[0:NH, (kb - k4) * 128:(kb - k4 + 1) * 128], cum[:, kb, :], cst[:, 0, :])
        nb4 = min(NKB, k4 + 4) - k4
        P.copy('dve', cumT[:, k4 * 128:(k4 + nb4) * 128], psm[0:NH, 0:nb4 * 128])

    def attend(h_, qv, nq, blocks, out_rows, cqv, done=None):
        base = 64 * (h_ % 2)
        po = pso()
        pd = psd()
        qzb = qz[h_ % 2][qzc[0] % 2]
        if h_ % 2 == 1:
            qzc[0] += 1
        P.copy('pool', qzb[base:base + 64, 0:nq], qv(0, nq))
        DEPTH = 2
        pts = {}
        nb_ = len(blocks)
        for bi in range(nb_ + DEPTH):
            if bi < nb_:
                kv, vv, bv, nk, q0, masked = blocks[bi]
                n = nq - q0
                p_ = pss()
                P.mm(p_[0:nk, 0:n], kv, qzb[:, q0:nq], start=True, stop=(cqv is None))
                if cqv is not None:
                    P.mm(p_[0:nk, 0:n], sel[:, h_, 0:nk], cqv(q0, nq), start=False, stop=True)
                pt = pTr()
                P.act(pt[0:nk, 0:n], p_[0:nk, 0:n], AF.Exp, bias=bv[:, h_:h_ + 1], scale=SCALE)
                if masked:
                    m_ = min(nk, n)
                    P.tt('pool', pt[0:nk, 0:m_], pt[0:nk, 0:m_], trib[0:nk, 0:m_], ALU.mult)
                pts[bi] = pt
            bj = bi - DEPTH
            if bj >= 0:
                kv, vv, bv, nk, q0, masked = blocks[bj]
                n = nq - q0
                pt = pts.pop(bj)
                P.mm(po[:, q0:nq], vv, pt[0:nk, 0:n], start=(bj == 0), stop=(bj == nb_ - 1))
                P.mm(pd[:, q0:nq], onesb[0:nk, :], pt[0:nk, 0:n], start=(bj == 0), stop=(bj == nb_ - 1))
            if bi < nb_ + DEPTH - 1:
                yield
        rc = rec[h_ % 2]
        P.recip(rc[base:base + 64, 0:nq], pd[base:base + 64, 0:nq])
        P.tt('dve', out_rows[base:base + 64, 0:nq], po[base:base + 64, 0:nq], rc[base:base + 64, 0:nq], ALU.mult)
        if done is not None:
            done()

    def run_heads(specs):
        gens = []
        off = 0
        for (args, nb_) in specs:
            gens.append([off, attend(*args), True])
            off += nb_
        step = 0
        while any(g_[2] for g_ in gens):
            for g_ in gens:
                if g_[2] and g_[0] <= step:
                    try:
                        next(g_[1])
                    except StopIteration:
                        g_[2] = False
            step += 1

    NQ = 512
    for qi in range(S // NQ):
        lastb = (qi + 1) * 4 - 1
        P.mm(psm[:, 0:NH], cst[:, 4, :], cum[:, lastb, :])
        P.copy('dve', Rb[:], psm[:, 0:NH])
        qe = (qi + 1) * NQ
        bq = biasq[qi % 2]
        P.tt('dve', bq[:, 0:lastb + 1, :], negc[:, 0:lastb + 1, :],
             Rb.v(Rb.t[:, None, :].to_broadcast([128, lastb + 1, NH])), ALU.add)
        P.ts('dve', cqs[0:NH, qi * NQ:qe], cumT[:, qi * NQ:qe], cumT[:, qe - 1:qe], ALU.subtract, 1.0 / SCALE, ALU.mult)
        specs = []
        for p2 in range(4):
            o_ = ob[p2 % 2]
            for h_ in (2 * p2, 2 * p2 + 1):
                base = 64 * (h_ % 2)
                blocks = []
                for kb in range(lastb + 1):
                    m_ = kb - 4 * qi
                    q0 = max(0, m_) * 128
                    blocks.append((kT[:, p2, kb * 128:(kb + 1) * 128],
                                   vF[:, kb, p2 * 128:(p2 + 1) * 128], bq[:, kb, :], 128, q0, m_ >= 0))
                done = None
                if h_ % 2 == 1:
                    done = (lambda p2=p2, qi=qi, o_=o_:
                            P.dma('sp', g.obT.v(obv[:, p2, qi * NQ:(qi + 1) * NQ]), o_[:, 0:NQ], csr()))
                specs.append(((h_, lambda q0, nq, base=base, p2=p2, qi=qi: qT[base:base + 64, p2, qi * NQ + q0:qi * NQ + nq],
                               NQ, blocks, o_, lambda q0, nq, qi=qi: cqs[:, qi * NQ + q0:qi * NQ + nq], done), len(blocks)))
        run_heads(specs)

    kc = P.sbuf("kc", [128, NPB, 512], F32)
    kTc = P.sbuf("kTc", [128, 4, PAST], BF16)
    vc = P.sbuf("vc", [128, NPB, 512], BF16)
    lfc = P.sbuf("lfc", [128, NPB + 1, NH], F32)
    cumc = P.sbuf("cumc", [128, NPB + 1, NH], F32)
    negcc = P.sbuf("negcc", [128, NPB + 1, NH], F32)
    qTs = P.sbuf("qTs", [128, 4, NS * LS], BF16)
    kTs = P.sbuf("kTs", [128, 4, NS * LS], BF16)
    vFs = P.sbuf("vFs", [LS, NS, 512], BF16)
    obs = P.sbuf("obs", [128, 4, NS * LS], BF16)
    P.dma('sp', qTs[:], g.qT.v(g.qT.t.rearrange("(c p) t -> p c t", p=128)[:, :, S:S + NS * LS]), cc_)
    P.dma('sp', kTs[:], g.kT.v(g.kT.t.rearrange("(c p) t -> p c t", p=128)[:, :, S:S + NS * LS]), cc_)
    P.dma('sp', vFs[:], g.vF.v(g.vF.t[S:S + NS * LS, :].rearrange("(s t) c -> t s c", t=LS)), cc_)
    cck = P.chan("ck")
    ccv = P.chan("cv")
    evr = Rot(['act', 'dve'])
    for s_ in range(NS):
        P.dma('sp', kc[:], V(g.cache_k.t[l, s_].rearrange("(b p) c -> p b c", p=128), (g.cache_k.tok,)), cck)
        P.dma('pool', vc[:], V(g.cache_v.t[l, s_].rearrange("(b p) c -> p b c", p=128), (g.cache_v.tok,)), ccv)
        P.dma('sp', lfc[:, 0:NPB, :], V(g.cache_lf.t[l, s_].rearrange("(b p) c -> p b c", p=128), (g.cache_lf.tok,)), cck)
        P.dma('sp', lfc[0:LS, NPB, :], g.lf[S + s_ * LS:S + (s_ + 1) * LS, :], cck)
        for kb in range(NPB):
            p_ = pss()
            for p2 in range(4):
                P.tr(p_[:, p2 * 128:(p2 + 1) * 128], kc[:, kb, p2 * 128:(p2 + 1) * 128], cst[:, 0, :])
            P.copy(evr(), kTc[:, :, kb * 128:(kb + 1) * 128], p_.v(p_.t[:, :].rearrange("p (j c) -> p j c", j=4)))
        cumsum_blocks(lambda b_, n_: lfc[0:n_, b_, :], NPB + 1, LS, lambda b_, n_: cumc[0:n_, b_, :], None)
        P.ts('dve', negcc[:], cumc[:], -1.0, ALU.mult)
        P.mm(psm[:, 0:NH], cst[0:LS, 8, :], cumc[0:LS, NPB, :])
        P.copy('dve', Rb[:], psm[:, 0:NH])
        P.tt('dve', biass[:], negcc[:], Rb.v(Rb.t[:, None, :].to_broadcast([128, NPB + 1, NH])), ALU.add)
        specs = []
        for h_ in range(NH):
            p2 = h_ // 2
            base = 64 * (h_ % 2)
            blocks = []
            for kb in range(NPB):
                blocks.append((kTc[:, p2, kb * 128:(kb + 1) * 128], vc[:, kb, p2 * 128:(p2 + 1) * 128],
                               biass[:, kb, :], 128, 0, False))
            blocks.append((kTs[:, p2, s_ * LS:(s_ + 1) * LS], vFs[0:LS, s_, p2 * 128:(p2 + 1) * 128],
                           biass[0:LS, NPB, :], LS, 0, True))
            specs.append(((h_, lambda q0, nq, base=base, p2=p2, s_=s_: qTs[base:base + 64, p2, s_ * LS + q0:s_ * LS + nq],
                           LS, blocks, obs.v(obs.t[:, p2, s_ * LS:(s_ + 1) * LS]), None, None), len(blocks)))
        run_heads(specs)
    P.dma('sp', g.obT.v(obv[:, :, S:S + NS * LS]), obs[:], csr())
    P.emit()


def phase_C1(nc, g, cfg, l):
    P = Prog(nc, "C1%d" % l)
    cw = P.chan("w")
    cl = [P.chan("l%d" % i) for i in range(2)]
    cs = [P.chan("s%d" % i) for i in range(2)]
    pa = P.sbuf("pa", [128, 4, D], BF16)
    pb = P.sbuf("pb", [128, 4, D], BF16)
    wo = P.sbuf("wo", [128, 8, D], BF16)
    load_w_bf16(P, pa, g.p_a.t[l], 4, D, cw)
    load_w_bf16(P, pb, g.p_b.t[l], 4, D, cw)
    load_w_bf16(P, wo, g.w_out.t[l], 8, D, cw)
    xt = [P.sbuf("xt%d" % i, [128, 8, 512], F32) for i in range(2)]
    oa = [P.sbuf("oa%d" % i, [128, 4, 512], BF16) for i in range(2)]
    ob = [P.sbuf("ob%d" % i, [128, 4, 512], BF16) for i in range(2)]
    ga = [P.sbuf("ga%d" % i, [128, 8, 512], BF16) for i in range(2)]
    gb = [P.sbuf("gb%d" % i, [128, 8, 512], BF16) for i in range(2)]
    ma = [P.sbuf("ma%d" % i, [128, 512], F32) for i in range(2)]
    mar = Rot(ma)
    mT = P.sbuf("mT", [128, 8, 512], BF16)
    ps = [P.psum("ps%d" % i, [128, 512]) for i in range(8)]
    psr = Rot(ps)
    v3 = lambda b_: b_.t.rearrange("(c p) t -> p c t", p=128)
    tiles = cfg.tiles

    def load(ti):
        t0, nt = tiles[ti][0], tiles[ti][1]
        b = ti % 2
        P.dma('sp', xt[b][:, :, 0:nt], g.xT.v(v3(g.xT)[:, :, t0:t0 + nt]), cl[b])
        P.dma('sp', oa[b][:, :, 0:nt], g.oaT.v(v3(g.oaT)[:, :, t0:t0 + nt]), cl[b])
        P.dma('sp', ob[b][:, :, 0:nt], g.obT.v(v3(g.obT)[:, :, t0:t0 + nt]), cl[b])
        P.dma('sp', ga[b][:, :, 0:nt], g.gaT.v(v3(g.gaT)[:, :, t0:t0 + nt]), cl[b])
        P.dma('sp', gb[b][:, :, 0:nt], g.gbT.v(v3(g.gbT)[:, :, t0:t0 + nt]), cl[b])

    load(0)
    for ti, (t0, nt, nseq, L, C) in enumerate(tiles):
        if ti + 1 < len(tiles):
            load(ti + 1)
        b = ti % 2
        for c in range(8):
            p1 = psr()
            for k in range(4):
                P.mm(p1[:, 0:nt], pa[:, k, c * 128:(c + 1) * 128], oa[b][:, k, 0:nt], start=(k == 0), stop=(k == 3))
            m_ = mar()
            P.tt('dve', m_[:, 0:nt], p1[:, 0:nt], ga[b][:, c, 0:nt], ALU.mult)
            p2 = psr()
            for k in range(4):
                P.mm(p2[:, 0:nt], pb[:, k, c * 128:(c + 1) * 128], ob[b][:, k, 0:nt], start=(k == 0), stop=(k == 3))
            P.tt('dve', mT[:, c, 0:nt], p2[:, 0:nt], gb[b][:, c, 0:nt], ALU.mult)
            P.tt('pool', mT[:, c, 0:nt], mT[:, c, 0:nt], m_[:, 0:nt], ALU.add)
        for c in range(8):
            p1 = psr()
            for k in range(8):
                P.mm(p1[:, 0:nt], wo[:, k, c * 128:(c + 1) * 128], mT[:, k, 0:nt], start=(k == 0), stop=(k == 7))
            P.tt('dve', xt[b][:, c, 0:nt], xt[b][:, c, 0:nt], p1[:, 0:nt], ALU.add)
        P.dma('sp', g.xT.v(v3(g.xT)[:, :, t0:t0 + nt]), xt[b][:, :, 0:nt], cs[b])
    P.emit()


def phase_C2(nc, g, cfg, l):
    P = Prog(nc, "C2%d" % l)
    cw = P.chan("w")
    cc_ = P.chan("c")
    cl = [P.chan("l%d" % i) for i in range(2)]
    cs = [P.chan("s%d" % i) for i in range(2)]
    cst = load_consts(P, g, cc_)
    pv = P.sbuf("pv", [128, PV_N], F32)
    P.dma('sp', pv[:], g.pvec[l], cc_)
    wg = WBuf(P, "wg", 8, DFF, 512)
    wu = WBuf(P, "wu", 8, DFF, 512)
    wd = P.sbuf("wd", [128, NFF, D], BF16)
    chs = [P.chan("w%d" % i) for i in range(8)]
    i_ = 0
    for c0 in range(0, DFF, 512):
        c1 = min(DFF, c0 + 512)
        for (dst_, src_) in ((wg, g.w_gate), (wu, g.w_up)):
            for k in range(8):
                P.dma('pool', dst_[:, k, c0:c1], V(src_.t[l][k * 128:(k + 1) * 128, c0:c1], ()), chs[i_ % 8])
                i_ += 1
    load_w_bf16(P, wd, g.w_down.t[l], NFF, D, cw)
    NT = 256
    xt = [P.sbuf("xt%d" % i, [128, 8, NT], F32) for i in range(2)]
    sqb = [P.sbuf("sq%d" % i, [128, NT], F32) for i in range(2)]
    sqr = Rot(sqb)
    rs = P.sbuf("rs", [128, NT], F32)
    hT = P.sbuf("hT", [128, 8, NT], BF16)
    sl = [P.sbuf("sl%d" % i, [128, NT], F32) for i in range(2)]
    slr = Rot(sl)
    hid = P.sbuf("hid", [128, NFF, NT], BF16)
    ps = [P.psum("ps%d" % i, [128, 512]) for i in range(8)]
    psr = Rot(ps[0:7])
    ps_ss = ps[7]
    evr = Rot(['dve', 'pool'])
    v3 = g.xT.t.rearrange("(c p) t -> p c t", p=128)
    tiles = []
    for (t0, nt, _, _, _) in cfg.tiles:
        for o in range(0, nt, NT):
            tiles.append((t0 + o, min(NT, nt - o)))
    P.dma('sp', xt[0][:, :, 0:tiles[0][1]], g.xT.v(v3[:, :, tiles[0][0]:tiles[0][0] + tiles[0][1]]), cl[0])
    for ti, (t0, nt) in enumerate(tiles):
        if ti + 1 < len(tiles):
            t0n, ntn = tiles[ti + 1]
            P.dma('sp', xt[(ti + 1) % 2][:, :, 0:ntn], g.xT.v(v3[:, :, t0n:t0n + ntn]), cl[(ti + 1) % 2])
        x = xt[ti % 2]
        rmsnorm_tile(P, x, nt, PV_G2, pv, cst, ps_ss, sqr, rs, hT, evr)
        for f in range(NFF):
            pg = psr()
            pu = psr()
            for k in range(8):
                P.mm(pg[:, 0:nt], wg[:, k, f * 128:(f + 1) * 128], hT[:, k, 0:nt], start=(k == 0), stop=(k == 7))
            for k in range(8):
                P.mm(pu[:, 0:nt], wu[:, k, f * 128:(f + 1) * 128], hT[:, k, 0:nt], start=(k == 0), stop=(k == 7))
            s_ = slr()
            P.act(s_[:, 0:nt], pg[:, 0:nt], AF.Silu)
            P.tt('dve', hid[:, f, 0:nt], pu[:, 0:nt], s_[:, 0:nt], ALU.mult)
        for c in range(8):
            p1 = psr()
            for f in range(NFF):
                P.mm(p1[:, 0:nt], wd[:, f, c * 128:(c + 1) * 128], hid[:, f, 0:nt], start=(f == 0), stop=(f == NFF - 1))
            P.tt('dve', x[:, c, 0:nt], x[:, c, 0:nt], p1[:, 0:nt], ALU.add)
        P.dma('sp', g.xT.v(v3[:, :, t0:t0 + nt]), x[:, :, 0:nt], cs[ti % 2])
    P.emit()


def phase_F(nc, g, cfg):
    P = Prog(nc, "F")
    cc_ = P.chan("c")
    cl = [P.chan("l%d" % i) for i in range(2)]
    cs = [P.chan("s%d" % i) for i in range(2)]
    cst = load_consts(P, g, cc_)
    pv = P.sbuf("pv", [128, 8], F32)
    P.dma('sp', pv[:], g.fing[:], cc_)
    xt = [P.sbuf("xt%d" % i, [128, 8, 128], F32) for i in range(2)]
    sqb = [P.sbuf("sq%d" % i, [128, 128], F32) for i in range(2)]
    sqr = Rot(sqb)
    rs = P.sbuf("rs", [128, 128], F32)
    hT = P.sbuf("hT", [128, 8, 128], F32)
    yo = [P.sbuf("yo%d" % i, [128, D], F32) for i in range(2)]
    ps = [P.psum("ps%d" % i, [128, 512]) for i in range(5)]
    psr = Rot(ps[0:4])
    ps_ss = ps[4]
    evr = Rot(['dve', 'pool'])
    ev2 = Rot(['act', 'dve'])
    v3 = g.xT.t.rearrange("(c p) t -> p c t", p=128)
    blocks = [(g.y_prompt, i * 128, 128, i * 128) for i in range(cfg.seq // 128)] + [(g.y_sample, 0, NS * LS, cfg.seq)]
    P.dma('sp', xt[0][:, :, 0:blocks[0][2]], g.xT.v(v3[:, :, 0:blocks[0][2]]), cl[0])
    for bi, (dst, r0, nr, t0) in enumerate(blocks):
        if bi + 1 < len(blocks):
            _, _, nrn, t0n = blocks[bi + 1]
            P.dma('sp', xt[(bi + 1) % 2][:, :, 0:nrn], g.xT.v(v3[:, :, t0n:t0n + nrn]), cl[(bi + 1) % 2])
        x = xt[bi % 2]
        rmsnorm_tile(P, x, nr, 0, pv, cst, ps_ss, sqr, rs, hT, evr)
        y = yo[bi % 2]
        for hh in range(2):
            p_ = psr()
            for c in range(4):
                P.tr(p_[0:nr, c * 128:(c + 1) * 128], hT[:, hh * 4 + c, 0:nr], cst[:, 0, :])
            P.copy(ev2(), y[0:nr, hh * 512:(hh + 1) * 512], p_[0:nr, :])
        P.dma('sp', dst[r0:r0 + nr, :], y[0:nr, :], cs[bi % 2])
    P.emit()


def build(cfg, phases=None):
    nc = bass.Bass("TRN2", target_bir_lowering=False)
    nc._sempool = SemPool(nc)
    g = declare(nc, cfg)
    ph = getattr(cfg, 'phases', None) or ['T0', 'A', 'BR', 'BF', 'C1', 'C2', 'F']
    if 'T0' in ph:
        phase_T0(nc, g, cfg)
    for l in range(cfg.depth):
        if 'A' in ph:
            phase_A(nc, g, cfg, l)
        if 'BR' in ph:
            phase_BR(nc, g, cfg, l)
        if 'BF' in ph:
            phase_BF(nc, g, cfg, l)
        if 'C1' in ph:
            phase_C1(nc, g, cfg, l)
        if 'C2' in ph:
            phase_C2(nc, g, cfg, l)
    if 'F' in ph:
        phase_F(nc, g, cfg)
    nc._sempool.stack.close()
    return nc


def make_consts():
    c = np.zeros((128, 9, 128), np.float32)
    i = np.arange(128)
    c[:, 0, :] = np.eye(128)
    c[:, 1, :] = 1.0
    c[:, 2, :] = (i[:, None] // 64 == i[None, :] // 64)
    c[:, 3, :] = (i[:, None] <= i[None, :])
    c[127, 4, :] = 1.0
    c[:, 5, :] = (i[:, None] < i[None, :])
    c[:, 6, :] = (i[:, None] > i[None, :])
    c[:, 7, :] = (i[:, None] <= i[None, :])
    c[LS - 1, 8, :] = 1.0
    hs = np.zeros((128, 2), np.float32)
    hs[0:64, 0] = 1.0
    hs[64:128, 1] = 1.0
    return c, hs


def col(v, n):
    return np.ascontiguousarray(np.asarray(v, np.float32).reshape(n, 128).T)


def kernel(x_prompt, x_sample, cache_fox_k, cache_fox_v, cache_fox_logf, state_rwkv, state_shift,
           norm1_g, w_in, rwkv_mu, rwkv_w0, rwkv_w2, rwkv_a0, rwkv_a2, rwkv_g2, rwkv_k_k, rwkv_k_a,
           rwkv_r_k, rwkv_lnx_g, rwkv_lnx_b, fox_bf, p_a, p_b, w_out, norm2_g, w_gate, w_up, w_down,
           final_g, _cfg=None):
    f = lambda a: np.ascontiguousarray(np.asarray(a, dtype=np.float32))
    x_prompt = f(x_prompt)
    L = w_in.shape[0]
    B, S = x_prompt.shape[0], x_prompt.shape[1]
    DB = x_sample.shape[0]
    PAST = cache_fox_k.shape[2]
    cfg = _cfg or Cfg(S, L, PAST)
    nc = build(cfg)
    consts, hsel = make_consts()
    pvec = np.zeros((L, 128, PV_N), np.float32)
    prow = np.zeros((L, 3, 512), np.float32)
    for l in range(L):
        pvec[l, :, PV_G1:PV_G1 + 8] = col(norm1_g[l], 8)
        pvec[l, :, PV_MU:PV_MU + 14] = col(rwkv_mu[l], 14)
        pvec[l, :, PV_W0:PV_W0 + 4] = col(rwkv_w0[l], 4)
        pvec[l, :, PV_A0:PV_A0 + 4] = col(rwkv_a0[l], 4)
        pvec[l, :, PV_KK:PV_KK + 4] = col(rwkv_k_k[l], 4)
        pvec[l, :, PV_KA:PV_KA + 4] = col(rwkv_k_a[l], 4)
        pvec[l, :, PV_RK:PV_RK + 4] = col(np.asarray(rwkv_r_k[l]).reshape(-1), 4)
        pvec[l, :, PV_G2:PV_G2 + 8] = col(norm2_g[l], 8)
        prow[l, 0] = np.asarray(rwkv_lnx_g[l])
        prow[l, 1] = np.asarray(rwkv_lnx_b[l])
        prow[l, 2, 0:NH] = np.asarray(fox_bf[l])
    shared = dict(w_in=f(w_in), w2=f(rwkv_w2), a2=f(rwkv_a2), g2=f(rwkv_g2), p_a=f(p_a), p_b=f(p_b),
                  w_out=f(w_out), w_gate=f(w_gate), w_up=f(w_up), w_down=f(w_down), pvec=pvec, prow=prow,
                  fing=col(final_g, 8), consts=consts, hsel=hsel)
    ck = f(cache_fox_k).reshape(L, DB, PAST, 512)
    cv = f(cache_fox_v).reshape(L, DB, PAST, 512)
    clf = f(cache_fox_logf)
    sr = f(state_rwkv)
    ssh = f(state_shift).reshape(L, DB, RC)
    xs = f(x_sample)
    in_maps = []
    for c in range(8):
        b = c // 2
        sl = slice(c * NS, (c + 1) * NS)
        m = dict(shared)
        m.update(x_prompt=x_prompt[b], x_sample=np.ascontiguousarray(xs[sl].reshape(NS * LS, D)),
                 cache_k=np.ascontiguousarray(ck[:, sl]), cache_v=np.ascontiguousarray(cv[:, sl]),
                 cache_lf=np.ascontiguousarray(clf[:, sl]), state_in=np.ascontiguousarray(sr[:, sl]),
                 shift_in=np.ascontiguousarray(ssh[:, sl]))
        in_maps.append(m)
    if getattr(cfg, 'trace', False):
        res = run_bass_kernel_spmd(nc, in_maps, core_ids=list(range(8)), trace=True)
        print("EXEC_TIME_NS", res.exec_time_ns)
    else:
        res = run_bass_kernel_spmd(nc, in_maps, core_ids=list(range(8)))
    R = res.results
    ev = [R[2 * b] for b in range(B)]
    y_prompt = np.stack([r["y_prompt"] for r in ev])
    y_sample = np.concatenate([r["y_sample"].reshape(NS, LS, D) for r in R])
    kp = np.stack([r["kp"] for r in ev], 1).reshape(L, B, S, NH, HD)
    vp = np.stack([r["vp"] for r in ev], 1).reshape(L, B, S, NH, HD)
    lfp = np.stack([r["lfp"] for r in ev], 1)
    sp = np.stack([r["sp"] for r in ev], 1)
    shp = np.stack([r["shp"] for r in ev], 1).reshape(L, B, 1, RC)
    kd = np.concatenate([r["kd"].reshape(L, NS, LS, NH, HD) for r in R], 1)
    vd = np.concatenate([r["vd"].reshape(L, NS, LS, NH, HD) for r in R], 1)
    lfd = np.concatenate([r["lfd"].reshape(L, NS, LS, NH) for r in R], 1)
    sd = np.concatenate([r["sd"] for r in R], 1)
    shd = np.concatenate([r["shd"] for r in R], 1).reshape(L, DB, 1, RC)
    return (y_prompt, y_sample, kp, vp, lfp, sp, shp, kd, vd, lfd, sd, shd)
```

```python
import contextlib
import numpy as np
import concourse.bass as bass
import concourse.mybir as mybir
from concourse.bass_utils import run_bass_kernel_spmd

F32 = mybir.dt.float32
BF16 = mybir.dt.bfloat16
AF = mybir.ActivationFunctionType
ALU = mybir.AluOpType
AX = mybir.AxisListType

ENGS = ['pe', 'act', 'dve', 'pool', 'sp']


class Tok:
    __slots__ = ('w', 'r', 'name')

    def __init__(self, name=''):
        self.w = None
        self.r = {}
        self.name = name


class V:
    __slots__ = ('ap', 'toks')

    def __init__(self, ap, toks):
        self.ap = ap
        self.toks = toks

    def __getitem__(self, idx):
        return V(self.ap[idx], self.toks)


class Buf:
    def __init__(self, t, name):
        self.t = t
        self.name = name
        self.tok = Tok(name)

    def __getitem__(self, idx):
        return V(self.t[idx], (self.tok,))

    def v(self, ap):
        return V(ap, (self.tok,))


class Chan:
    def __init__(self, sem):
        self.sem = sem
        self.count = 0
        self.last = None


class Op:
    __slots__ = ('eng', 'fn', 'deps', 'sig', 'sigval', 'chan')


class SemPool:
    def __init__(self, nc):
        self.nc = nc
        self.stack = contextlib.ExitStack()
        self.esem = {e: self.stack.enter_context(nc.semaphore("e_" + e)) for e in ENGS}
        self.ecount = {e: 0 for e in ENGS}
        self.chans = {}

    def chan(self, name):
        if name not in self.chans:
            self.chans[name] = Chan(self.stack.enter_context(self.nc.semaphore("c_" + name)))
        return self.chans[name]


class Prog:
    def __init__(self, nc, name):
        self.nc = nc
        self.name = name
        self.ops = {e: [] for e in ENGS}
        self.stack = contextlib.ExitStack()
        self.pool = nc._sempool
        self.esem = self.pool.esem
        self.chans = []
        self.n = 0

    def chan(self, name):
        c = self.pool.chan(name)
        if c not in self.chans:
            c.last = None
            self.chans.append(c)
        return c

    def sbuf(self, name, shape, dt):
        t = self.stack.enter_context(self.nc.sbuf_tensor(self.name + name, list(shape), dt))
        return Buf(t, name)

    def psum(self, name, shape, dt=F32):
        t = self.stack.enter_context(self.nc.psum_tensor(self.name + name, list(shape), dt))
        return Buf(t, name)

    def op(self, eng, fn, r=(), w=(), chan=None):
        o = Op()
        o.eng = eng
        o.fn = fn
        o.chan = chan
        o.sig = False
        o.sigval = None
        deps = {}
        rt = []
        for x in r:
            rt.extend(x.toks)
        wt = []
        for x in w:
            wt.extend(x.toks)
        for t in rt:
            if t.w is not None:
                deps[id(t.w)] = t.w
        for t in wt:
            if t.w is not None:
                deps[id(t.w)] = t.w
            for x in t.r.values():
                deps[id(x)] = x
        if chan is not None:
            if chan.last is not None:
                deps[id(chan.last)] = chan.last
            chan.last = o
            chan.count += 16
            o.sigval = chan.count
            o.sig = True
        dl = []
        for d in deps.values():
            if d is o:
                continue
            if d.chan is None:
                if d.eng == eng and eng == 'pe':
                    continue
                d.sig = True
            dl.append(d)
        o.deps = dl
        key = ('c', id(chan)) if chan is not None else eng
        for t in rt:
            t.r[key] = o
        for t in wt:
            t.w = o
            t.r = {}
        self.ops[eng].append(o)
        self.n += 1
        return o

    def emit(self):
        nc = self.nc
        for e in ENGS:
            lst = self.ops[e]
            if lst and lst[-1].chan is None:
                lst[-1].sig = True
            c = self.pool.ecount[e]
            for o in lst:
                if o.chan is None and o.sig:
                    c += 1
                    o.sigval = c
            self.pool.ecount[e] = c
        maxsig = dict(self.pool.ecount)
        ops = self.ops
        esem = self.esem
        chans = self.chans

        def run(e, h):
            waited = {}
            for o in ops[e]:
                for d in o.deps:
                    s = d.chan.sem if d.chan is not None else esem[d.eng]
                    k = id(s)
                    if waited.get(k, 0) < d.sigval:
                        h.wait_ge(s, d.sigval)
                        waited[k] = d.sigval
                inst = o.fn(h)
                if o.chan is not None:
                    inst.then_inc(o.chan.sem, 16)
                elif o.sig:
                    inst.then_inc(esem[e], 1)
            for x in ENGS:
                if maxsig[x] > 0 and waited.get(id(esem[x]), 0) < maxsig[x]:
                    h.wait_ge(esem[x], maxsig[x])
            for c in chans:
                if c.count > 0 and waited.get(id(c.sem), 0) < c.count:
                    h.wait_ge(c.sem, c.count)

        with nc.Block() as block:
            @block.tensor
            def _(h):
                run('pe', h)

            @block.scalar
            def _(h):
                run('act', h)

            @block.vector
            def _(h):
                run('dve', h)

            @block.gpsimd
            def _(h):
                run('pool', h)

            @block.sync
            def _(h):
                run('sp', h)
        self.stack.close()

    def dma(self, q, out, in_, chan, **kw):
        return self.op(q, lambda h: h.dma_start(out=out.ap, in_=in_.ap, **kw), r=[in_], w=[out], chan=chan)

    def mm(self, out, lhsT, rhs, start=True, stop=True):
        return self.op('pe', lambda h: h.matmul(out.ap, lhsT.ap, rhs.ap, start=start, stop=stop),
                       r=[lhsT, rhs], w=[out])

    def tr(self, out, in_, ident):
        return self.op('pe', lambda h: h.transpose(out.ap, in_.ap, ident.ap), r=[in_, ident], w=[out])

    def act(self, out, in_, func, bias=None, scale=None):
        kw = {}
        r = [in_]
        if bias is not None:
            if isinstance(bias, V):
                kw['bias'] = bias.ap
                r.append(bias)
            else:
                kw['bias'] = bias
        if scale is not None:
            if isinstance(scale, V):
                kw['scale'] = scale.ap
                r.append(scale)
            else:
                kw['scale'] = scale
        return self.op('act', lambda h: h.activation(out.ap, in_.ap, func, **kw), r=r, w=[out])

    def tt(self, eng, out, a, b, op):
        return self.op(eng, lambda h: h.tensor_tensor(out.ap, a.ap, b.ap, op), r=[a, b], w=[out])

    def ts(self, eng, out, a, s1, op0, s2=None, op1=None):
        r = [a]
        a1 = s1.ap if isinstance(s1, V) else s1
        if isinstance(s1, V):
            r.append(s1)
        a2 = s2.ap if isinstance(s2, V) else s2
        if isinstance(s2, V):
            r.append(s2)
        kw = {}
        if op1 is not None:
            kw['op1'] = op1
        return self.op(eng, lambda h: h.tensor_scalar(out.ap, a.ap, a1, a2, op0, **kw), r=r, w=[out])

    def stt(self, eng, out, a, s, b, op0, op1):
        r = [a, b]
        sa = s.ap if isinstance(s, V) else s
        if isinstance(s, V):
            r.append(s)
        return self.op('dve', lambda h: h.scalar_tensor_tensor(out.ap, a.ap, sa, b.ap, op0, op1), r=r, w=[out])

    def copy(self, eng, out, in_):
        if eng == 'act':
            return self.op('act', lambda h: h.copy(out.ap, in_.ap), r=[in_], w=[out])
        return self.op(eng, lambda h: h.tensor_copy(out.ap, in_.ap), r=[in_], w=[out])

    def memset(self, eng, out, val):
        return self.op(eng, lambda h: h.memset(out.ap, val), r=[], w=[out])

    def recip(self, out, in_):
        return self.op('dve', lambda h: h.reciprocal(out.ap, in_.ap), r=[in_], w=[out])

    def rsum(self, out, in_):
        return self.op('dve', lambda h: h.reduce_sum(out.ap, in_.ap, AX.X), r=[in_], w=[out])


class Rot:
    def __init__(self, items):
        self.items = items
        self.i = 0

    def __call__(self):
        x = self.items[self.i % len(self.items)]
        self.i += 1
        return x


D = 1024
HD = 64
NH = 8
WR = 512
RC = 1792
FC = 1544
PT = 5384
DFF = 2816
NFF = 22
EPS = 1e-6
GN_EPS = 64e-5
SCALE = HD ** -0.5
LWC = -0.6065306597126334
NS = 4
LS = 16
CQ = RC
CK = RC + 512
CV = RC + 1024
CFL = RC + 1536
CGA = RC + FC
CGB = CGA + D
PV_G1, PV_MU, PV_W0, PV_A0, PV_KK, PV_KA, PV_RK, PV_G2, PV_N = 0, 8, 22, 26, 30, 34, 38, 42, 50


class Cfg:
    def __init__(self, seq=4096, depth=4, past=1024):
        self.seq = seq
        self.depth = depth
        self.past = past
        self.ta = seq + NS * LS
        self.tiles = [(i * 512, 512, 1, 512, 64) for i in range(seq // 512)] + [(seq, NS * LS, NS, LS, LS)]
        self.tilesA = [(i * 256, 256, 1, 256, 64) for i in range(seq // 256)] + [(seq, NS * LS, NS, LS, LS)]


class G:
    pass


def dram(nc, name, shape, dt, kind="Internal"):
    return Buf(nc.dram_tensor(name, list(shape), dt, kind=kind).ap(), name)


def declare(nc, cfg):
    g = G()
    L = cfg.depth
    S = cfg.seq
    TA = cfg.ta
    I = "ExternalInput"
    O = "ExternalOutput"
    g.x_prompt = dram(nc, "x_prompt", [S, D], F32, I)
    g.x_sample = dram(nc, "x_sample", [NS * LS, D], F32, I)
    g.cache_k = dram(nc, "cache_k", [L, NS, cfg.past, 512], F32, I)
    g.cache_v = dram(nc, "cache_v", [L, NS, cfg.past, 512], F32, I)
    g.cache_lf = dram(nc, "cache_lf", [L, NS, cfg.past, NH], F32, I)
    g.state_in = dram(nc, "state_in", [L, NS, NH, HD, HD], F32, I)
    g.shift_in = dram(nc, "shift_in", [L, NS, RC], F32, I)
    g.w_in = dram(nc, "w_in", [L, D, PT], F32, I)
    g.w2 = dram(nc, "w2", [L, 64, WR], F32, I)
    g.a2 = dram(nc, "a2", [L, 64, WR], F32, I)
    g.g2 = dram(nc, "g2", [L, 128, WR], F32, I)
    g.p_a = dram(nc, "p_a", [L, WR, D], F32, I)
    g.p_b = dram(nc, "p_b", [L, WR, D], F32, I)
    g.w_out = dram(nc, "w_out", [L, D, D], F32, I)
    g.w_gate = dram(nc, "w_gate", [L, D, DFF], F32, I)
    g.w_up = dram(nc, "w_up", [L, D, DFF], F32, I)
    g.w_down = dram(nc, "w_down", [L, DFF, D], F32, I)
    g.pvec = dram(nc, "pvec", [L, 128, PV_N], F32, I)
    g.prow = dram(nc, "prow", [L, 3, 512], F32, I)
    g.fing = dram(nc, "fing", [128, 8], F32, I)
    g.consts = dram(nc, "consts", [128, 9, 128], F32, I)
    g.hsel = dram(nc, "hsel", [128, 2], F32, I)
    g.y_prompt = dram(nc, "y_prompt", [S, D], F32, O)
    g.y_sample = dram(nc, "y_sample", [NS * LS, D], F32, O)
    g.kp = dram(nc, "kp", [L, S, 512], F32, O)
    g.vp = dram(nc, "vp", [L, S, 512], F32, O)
    g.lfp = dram(nc, "lfp", [L, S, NH], F32, O)
    g.sp_ = dram(nc, "sp", [L, NH, HD, HD], F32, O)
    g.shp = dram(nc, "shp", [L, RC], F32, O)
    g.kd = dram(nc, "kd", [L, NS * LS, 512], F32, O)
    g.vd = dram(nc, "vd", [L, NS * LS, 512], F32, O)
    g.lfd = dram(nc, "lfd", [L, NS * LS, NH], F32, O)
    g.sd = dram(nc, "sd", [L, NS, NH, HD, HD], F32, O)
    g.shd = dram(nc, "shd", [L, NS, RC], F32, O)
    g.xT = dram(nc, "s_xT", [D, TA], F32)
    for nm in ("rt", "at", "bt", "kt", "cum"):
        setattr(g, nm, dram(nc, "s_" + nm, [WR, TA], F32))
    g.vR = dram(nc, "s_vR", [TA, WR], F32)
    g.gR = dram(nc, "s_gR", [TA, WR], F32)
    g.bnR = dram(nc, "s_bnR", [TA, NH], F32)
    g.qT = dram(nc, "s_qT", [512, TA], BF16)
    g.kT = dram(nc, "s_kT", [512, TA], BF16)
    g.vF = dram(nc, "s_vF", [TA, 512], BF16)
    g.lf = dram(nc, "s_lf", [TA, NH], F32)
    g.gaT = dram(nc, "s_gaT", [D, TA], BF16)
    g.gbT = dram(nc, "s_gbT", [D, TA], BF16)
    g.oaT = dram(nc, "s_oaT", [512, TA], BF16)
    g.obT = dram(nc, "s_obT", [512, TA], BF16)
    return g


def load_consts(P, g, ch):
    c = P.sbuf("consts", [128, 9, 128], F32)
    P.dma('sp', c[:], g.consts[:], ch)
    return c


class WBuf:
    def __init__(self, P, name, nk, ncols, piece):
        self.b = P.sbuf(name, [128, nk, ncols], BF16)
        self.t = self.b.t
        self.piece = piece
        self.toks = [[Tok("%s_%d_%d" % (name, i, k)) for k in range(nk)] for i in range((ncols + piece - 1) // piece)]

    def __getitem__(self, idx):
        cs = idx[-1]
        k = idx[1]
        assert isinstance(cs, slice) and isinstance(k, int)
        p0 = cs.start // self.piece
        p1 = (cs.stop - 1) // self.piece
        return V(self.t[idx], tuple(self.toks[p][k] for p in range(p0, p1 + 1)))


def load_w_bf16(P, dst, src_ap, nk, ncols, ch):
    chs = [P.chan("w%d" % i) for i in range(8)]
    if isinstance(dst, WBuf):
        i = 0
        for c0 in range(0, ncols, dst.piece):
            c1 = min(ncols, c0 + dst.piece)
            for k in range(nk):
                P.dma('pool', dst[:, k, c0:c1], V(src_ap[k * 128:(k + 1) * 128, c0:c1], ()), chs[i % 8])
                i += 1
        return
    i = 0
    for k in range(nk):
        for c0 in range(0, ncols, 2048):
            c1 = min(ncols, c0 + 2048)
            P.dma('pool', dst[:, k, c0:c1], V(src_ap[k * 128:(k + 1) * 128, c0:c1], ()), chs[i % 8])
            i += 1


def rmsnorm_tile(P, x, nt, gcol, pv, cst, ps_ss, sqr, rs, hT, evr):
    for c in range(8):
        sq = sqr()
        P.act(sq[:, 0:nt], x[:, c, 0:nt], AF.Square)
        P.mm(ps_ss[:, 0:nt], cst[:, 1, :], sq[:, 0:nt], start=(c == 0), stop=(c == 7))
    P.act(rs[:, 0:nt], ps_ss[:, 0:nt], AF.Sqrt, bias=EPS, scale=1.0 / D)
    P.recip(rs[:, 0:nt], rs[:, 0:nt])
    for c in range(8):
        P.stt(evr(), hT[:, c, 0:nt], x[:, c, 0:nt], pv[:, gcol + c:gcol + c + 1], rs[:, 0:nt], ALU.mult, ALU.mult)


def phase_T0(nc, g, cfg):
    P = Prog(nc, "T0")
    ch = [P.chan("l%d" % i) for i in range(2)] + [P.chan("s%d" % i) for i in range(2)]
    cst = load_consts(P, g, P.chan("c"))
    xin = [P.sbuf("xin%d" % i, [128, D], F32) for i in range(2)]
    xo = [P.sbuf("xo%d" % i, [128, 8, 128], F32) for i in range(2)]
    ps = [P.psum("ps%d" % i, [128, 4, 128]) for i in range(4)]
    psr = Rot(ps)
    blocks = [(g.x_prompt, i * 128, 128, i * 128) for i in range(cfg.seq // 128)] + [(g.x_sample, 0, NS * LS, cfg.seq)]
    xTv = g.xT.t.rearrange("(c p) t -> p c t", p=128)
    evr = Rot(['dve', 'act'])
    for bi, (src, r0, nr, t0) in enumerate(blocks):
        xi = xin[bi % 2]
        xx = xo[bi % 2]
        P.dma('sp', xi[0:nr, :], src[r0:r0 + nr, :], ch[bi % 2])
        for hh in range(2):
            p_ = psr()
            for c in range(4):
                cc = hh * 4 + c
                P.tr(p_[:, c, 0:nr], xi[0:nr, cc * 128:(cc + 1) * 128], cst[0:nr, 0, 0:nr])
            P.copy(evr(), xx[:, hh * 4:hh * 4 + 4, 0:nr], p_[:, :, 0:nr])
        P.dma('sp', g.xT.v(xTv[:, :, t0:t0 + nr]), xx[:, :, 0:nr], ch[2 + bi % 2])
    P.emit()


def phase_A(nc, g, cfg, l):
    P = Prog(nc, "A%d" % l)
    S = cfg.seq
    cl = [P.chan("l%d" % i) for i in range(2)]
    cs = [P.chan("s%d" % i) for i in range(6)]
    csr = Rot(cs)
    cw = P.chan("w")
    cc_ = P.chan("c")
    cst = load_consts(P, g, cc_)
    pv = P.sbuf("pv", [128, PV_N], F32)
    P.dma('sp', pv[:], g.pvec[l], cc_)
    hsel = P.sbuf("hsel", [128, 2], F32)
    P.dma('sp', hsel[:], g.hsel[:], cc_)
    bfr = P.sbuf("bfr", [128, NH], F32)
    P.dma('sp', bfr[:], V(g.prow.t[l, 2:3, 0:NH].to_broadcast([128, NH]), g.prow.tok and (g.prow.tok,)), cc_)
    P.ts('dve', bfr[:], bfr[:], -1.0, ALU.mult)
    w = WBuf(P, "w", 8, PT, 512)
    load_w_bf16(P, w, g.w_in.t[l], 8, PT, cw)
    w2b = P.sbuf("w2b", [128, WR], BF16)
    P.memset('pool', w2b[64:128, :], 0.0)
    P.dma('pool', w2b[0:64, :], g.w2[l], cw)
    a2b = P.sbuf("a2b", [128, WR], BF16)
    P.memset('pool', a2b[0:64, :], 0.0)
    P.dma('pool', a2b[64:128, :], g.a2[l], cw)
    g2b = P.sbuf("g2b", [128, WR], BF16)
    P.dma('pool', g2b[:], g.g2[l], cw)

    xt = [P.sbuf("xt%d" % i, [128, 8, 256], F32) for i in range(2)]
    sqb = [P.sbuf("sq%d" % i, [128, 256], F32) for i in range(2)]
    sqr = Rot(sqb)
    rs = P.sbuf("rs", [128, 256], F32)
    hTs = [P.sbuf("hT%d" % i, [128, 8, 256], BF16) for i in range(2)]
    pr = P.sbuf("pr", [128, 14, 256], F32)
    sh = P.sbuf("sh", [128, 14, 256], F32)
    carry = P.sbuf("carry", [128, 14, NS], F32)
    u = sh
    msk = P.sbuf("msk", [128, 256], F32)
    obf = [P.sbuf("obf%d" % i, [128, 4, 256], BF16) for i in range(2)]
    obr = Rot(obf)
    tko = [P.sbuf("tko%d" % i, [128, 512], F32) for i in range(3)]
    tkr = Rot(tko)
    tkb = [P.sbuf("tkb%d" % i, [128, 512], BF16) for i in range(2)]
    tkbr = Rot(tkb)
    lfo = [P.sbuf("lfo%d" % i, [128, NH], F32) for i in range(2)]
    lfr = Rot(lfo)
    bno = [P.sbuf("bno%d" % i, [128, NH], F32) for i in range(2)]
    bnr = Rot(bno)
    nm = ["tw", "alb", "sgl", "sg", "cum", "cex", "G", "Gi", "Gx", "a", "kk", "kq", "rsq", "kp", "t1",
          "o_rt", "o_at", "o_bt", "o_kt", "rk"]
    T = {}
    for n_ in nm:
        dt = BF16 if n_ in ("tw", "alb", "sgl") else F32
        T[n_] = P.sbuf("r_" + n_, [128, 256], dt)
    T2 = {}
    for n_ in nm:
        if n_ in ("tw", "alb", "sgl"):
            continue
        T2[n_] = P.sbuf("r2_" + n_, [128, 256], F32)
    shT = P.sbuf("shT", [64, 512], F32)
    sin = P.sbuf("sin", [NS, RC], F32)
    ps = [P.psum("ps%d" % i, [128, 512]) for i in range(8)]
    psr = Rot(ps[0:6])
    ps_ss = ps[6]
    ps_m = ps[7]
    evr = Rot(['dve', 'pool'])
    ev2 = Rot(['act', 'dve'])

    xTv = g.xT.t.rearrange("(c p) t -> p c t", p=128)

    def proj(hT, pst, c0, ncol, nt):
        for k in range(8):
            P.mm(pst[0:ncol, 0:nt], w[:, k, c0:c0 + ncol], hT[:, k, 0:nt], start=(k == 0), stop=(k == 7))

    tiles = cfg.tilesA
    cur = [0]
    def stage1(ti):
        t0, nt, nseq, L, C = tiles[ti]
        x = xt[ti % 2]
        hT = hTs[ti % 2]
        sample = nseq > 1
        if ti + 1 < len(tiles):
            t0n, ntn = tiles[ti + 1][0], tiles[ti + 1][1]
            P.dma('sp', xt[(ti + 1) % 2][:, :, 0:ntn], g.xT.v(xTv[:, :, t0n:t0n + ntn]), cl[(ti + 1) % 2])
        rmsnorm_tile(P, x, nt, PV_G1, pv, cst, ps_ss, sqr, rs, hT, evr)
        yield
        for (c0, dst) in ((CQ, g.qT), (CK, g.kT)):
            ob_ = obr()
            for j in range(4):
                p_ = psr()
                proj(hT, p_, c0 + j * 128, 128, nt)
                P.copy(ev2(), ob_[:, j, 0:nt], p_[:, 0:nt])
            P.dma('sp', dst.v(dst.t.rearrange("(c p) t -> p c t", p=128)[:, :, t0:t0 + nt]), ob_[:, :, 0:nt], csr())
            yield
        for (c0, dst) in ((CGA, g.gaT), (CGB, g.gbT)):
            for hh in range(2):
                ob_ = obr()
                for j in range(4):
                    p_ = psr()
                    proj(hT, p_, c0 + (hh * 4 + j) * 128, 128, nt)
                    P.act(ob_[:, j, 0:nt], p_[:, 0:nt], AF.Sigmoid)
                P.dma('sp', dst.v(dst.t.rearrange("(c p) t -> p c t", p=128)[:, hh * 4:hh * 4 + 4, t0:t0 + nt]),
                      ob_[:, :, 0:nt], csr())
                yield
        for tb in range(0, nt, 128):
            nb = min(128, nt - tb)
            kout = (g.kd if sample else g.kp)
            vout = (g.vd if sample else g.vp)
            lout = (g.lfd if sample else g.lfp)
            ro = (0 if sample else t0) + tb
            for (c0, dst, isv) in ((CK, kout, False), (CV, vout, True)):
                p_ = psr()
                for k in range(8):
                    P.mm(p_[0:nb, :], hT[:, k, tb:tb + nb], w[:, k, c0:c0 + 512], start=(k == 0), stop=(k == 7))
                o_ = tkr()
                P.copy(ev2(), o_[0:nb, :], p_[0:nb, :])
                P.dma('sp', dst[l, ro:ro + nb, :], o_[0:nb, :], csr())
                if isv:
                    ob_ = tkbr()
                    P.copy('pool', ob_[0:nb, :], o_[0:nb, :])
                    P.dma('sp', g.vF[t0 + tb:t0 + tb + nb, :], ob_[0:nb, :], csr())
            p_ = psr()
            for k in range(8):
                P.mm(p_[0:nb, 0:NH], hT[:, k, tb:tb + nb], w[:, k, CFL:CFL + NH], start=(k == 0), stop=(k == 7))
            lf_ = lfr()
            P.stt('dve', lf_[0:nb, :], p_[0:nb, 0:NH], -1.0, bfr[0:nb, :], ALU.mult, ALU.add)
            P.act(lf_[0:nb, :], lf_[0:nb, :], AF.Exp)
            P.act(lf_[0:nb, :], lf_[0:nb, :], AF.Ln, bias=1.0)
            P.ts('dve', lf_[0:nb, :], lf_[0:nb, :], -1.0, ALU.mult)
            P.dma('sp', lout[l, ro:ro + nb, :], lf_[0:nb, :], csr())
            P.dma('sp', g.lf[t0 + tb:t0 + tb + nb, :], lf_[0:nb, :], csr())
            yield

    def stage2(ti):
        t0, nt, nseq, L, C = tiles[ti]
        x = xt[ti % 2]
        hT = hTs[ti % 2]
        sample = nseq > 1
        for j in range(14):
            p_ = psr()
            proj(hT, p_, j * 128, 128, nt)
            P.copy(ev2(), pr[:, j, 0:nt], p_[:, 0:nt])
            if j % 4 == 3:
                yield
        if sample:
            P.dma('sp', sin[:], g.shift_in[l], cc_)
            for j in range(14):
                P.tr(ps_m[:, j * NS:(j + 1) * NS], sin[0:NS, j * 128:(j + 1) * 128], cst[0:NS, 0, 0:NS])
            P.copy('dve', carry[:], ps_m.v(ps_m.t[:, 0:14 * NS].rearrange("p (c s) -> p c s", s=NS)))
        elif ti == 0:
            P.memset('pool', carry[:], 0.0)
        pr4 = pr.t[:, :, 0:nt].rearrange("p c (s l) -> p c s l", s=nseq)
        sh4 = sh.t[:, :, 0:nt].rearrange("p c (s l) -> p c s l", s=nseq)
        P.copy('pool', sh.v(sh4[:, :, :, 1:L]), pr.v(pr4[:, :, :, 0:L - 1]))
        P.copy('pool', sh.v(sh4[:, :, :, 0]), carry[:, :, 0:nseq])
        last_tile_of_seq = sample or (ti == len(tiles) - 2)
        if not sample:
            P.copy('pool', carry[:, :, 0:1], pr[:, :, nt - 1:nt])
        if last_tile_of_seq:
            for s_ in range(nseq):
                P.tr(ps_m[0:14, s_ * 128:(s_ + 1) * 128], pr.v(pr4[:, :, s_, L - 1]), cst[:, 0, :])
            P.copy('dve', shT.v(shT.t[0:14, 0:nseq * 128]), ps_m[0:14, 0:nseq * 128])
            if sample:
                dst = g.shd.v(g.shd.t[l].rearrange("s (c p) -> c s p", p=128))
                P.dma('sp', dst, shT.v(shT.t[0:14, 0:nseq * 128].rearrange("c (s p) -> c s p", s=nseq)), csr())
            else:
                dst = g.shp.v(g.shp.t[l].rearrange("(c p) -> c p", p=128))
                P.dma('sp', dst, shT[0:14, 0:128], csr())
        yield
        P.tt('dve', sh[:, :, 0:nt], sh[:, :, 0:nt], pr[:, :, 0:nt], ALU.subtract)
        for j in range(14):
            P.stt(evr(), u[:, j, 0:nt], sh[:, j, 0:nt], pv[:, PV_MU + j:PV_MU + j + 1], pr[:, j, 0:nt],
                  ALU.mult, ALU.add)
        if ti == 0 or sample:
            P.memset('pool', msk[:], 1.0)
            mv = msk.t[:, 0:nt].rearrange("p (c l) -> p c l", l=C)
            P.memset('pool', msk.v(mv[:, :, 0:1]), 0.0)
        yield
        P.act(T["tw"][0:64, 0:nt], u[0:64, 12, 0:nt], AF.Tanh)
        P.copy('pool', T["tw"][64:128, 0:nt], u[64:128, 12, 0:nt])
        P.act(T["sgl"][:, 0:nt], u[:, 13, 0:nt], AF.Sigmoid)
        def chain(j, TT):
            cs_ = slice(j * 128, (j + 1) * 128)
            r_ = u[:, j, 0:nt]
            k_ = u[:, 4 + j, 0:nt]
            p_ = psr()
            P.mm(p_[:, 0:nt], w2b[:, cs_], T["tw"][:, 0:nt])
            P.act(TT["sg"][:, 0:nt], p_[:, 0:nt], AF.Sigmoid, bias=pv[:, PV_W0 + j:PV_W0 + j + 1])
            P.op('dve', lambda h, o=TT["cum"].t[:, 0:nt], m=msk.t[:, 0:nt], s=TT["sg"].t[:, 0:nt]:
                 h.tensor_tensor_scan(o, m, s, 0.0, ALU.mult, ALU.add),
                 r=[msk[:], TT["sg"][:]], w=[TT["cum"][:]])
            P.tt('pool', TT["cex"][:, 0:nt], TT["cum"][:, 0:nt], TT["sg"][:, 0:nt], ALU.subtract)
            P.act(TT["G"][:, 0:nt], TT["cum"][:, 0:nt], AF.Exp, scale=LWC)
            P.act(TT["Gi"][:, 0:nt], TT["cum"][:, 0:nt], AF.Exp, scale=-LWC)
            P.act(TT["Gx"][:, 0:nt], TT["cex"][:, 0:nt], AF.Exp, scale=LWC)
            yield
            p2 = psr()
            P.mm(p2[:, 0:nt], a2b[:, cs_], T["tw"][:, 0:nt])
            P.act(TT["a"][:, 0:nt], p2[:, 0:nt], AF.Sigmoid, bias=pv[:, PV_A0 + j:PV_A0 + j + 1])
            P.ts('pool', TT["kk"][:, 0:nt], k_, pv[:, PV_KK + j:PV_KK + j + 1], ALU.mult)
            P.tt('pool', TT["kq"][:, 0:nt], TT["kk"][:, 0:nt], TT["kk"][:, 0:nt], ALU.mult)
            p3 = psr()
            P.mm(p3[:, 0:nt], cst[:, 2, :], TT["kq"][:, 0:nt])
            P.act(TT["rsq"][:, 0:nt], p3[:, 0:nt], AF.Sqrt, bias=1e-12, scale=1.0)
            P.recip(TT["rsq"][:, 0:nt], TT["rsq"][:, 0:nt])
            P.tt('dve', TT["kk"][:, 0:nt], TT["kk"][:, 0:nt], TT["rsq"][:, 0:nt], ALU.mult)
            P.ts('pool', TT["t1"][:, 0:nt], TT["a"][:, 0:nt], -1.0, ALU.add, pv[:, PV_KA + j:PV_KA + j + 1], ALU.mult)
            P.stt('dve', TT["kp"][:, 0:nt], TT["t1"][:, 0:nt], 1.0, k_, ALU.add, ALU.mult)
            yield
            P.tt('pool', TT["o_rt"][:, 0:nt], r_, TT["G"][:, 0:nt], ALU.mult)
            P.stt('dve', TT["o_at"][:, 0:nt], TT["kk"][:, 0:nt], -1.0, TT["Gx"][:, 0:nt], ALU.mult, ALU.mult)
            P.tt('pool', TT["o_bt"][:, 0:nt], TT["kk"][:, 0:nt], TT["a"][:, 0:nt], ALU.mult)
            P.tt('dve', TT["o_bt"][:, 0:nt], TT["o_bt"][:, 0:nt], TT["Gi"][:, 0:nt], ALU.mult)
            P.tt('pool', TT["o_kt"][:, 0:nt], TT["kp"][:, 0:nt], TT["Gi"][:, 0:nt], ALU.mult)
            P.stt('dve', TT["rk"][:, 0:nt], r_, pv[:, PV_RK + j:PV_RK + j + 1], TT["kp"][:, 0:nt], ALU.mult, ALU.mult)
            for nm_, dst in (("o_rt", g.rt), ("o_at", g.at), ("o_bt", g.bt), ("o_kt", g.kt), ("cum", g.cum)):
                P.dma("sp", dst[cs_, t0:t0 + nt], TT[nm_][:, 0:nt], csr())
            yield
            for tb in range(0, nt, 128):
                nb = min(128, nt - tb)
                P.mm(ps_m[0:nb, (tb // 128) * 8 + 2 * j:(tb // 128) * 8 + 2 * j + 2], TT["rk"][:, tb:tb + nb], hsel[:, :])
        for jp in (0, 2):
            gens = [chain(jp, T), chain(jp + 1, T2)]
            while gens:
                for gn in list(gens):
                    try:
                        next(gn)
                    except StopIteration:
                        gens.remove(gn)
                yield
        for tb in range(0, nt, 128):
            nb = min(128, nt - tb)
            bo = bnr()
            P.copy('dve', bo[0:nb, :], ps_m[0:nb, (tb // 128) * 8:(tb // 128) * 8 + 8])
            P.dma('sp', g.bnR[t0 + tb:t0 + tb + nb, :], bo[0:nb, :], csr())
            p_ = psr()
            for j in range(4):
                P.tr(p_[0:nb, j * 128:(j + 1) * 128], u[:, 8 + j, tb:tb + nb], cst[:, 0, :])
            o_ = tkr()
            P.copy(ev2(), o_[0:nb, :], p_[0:nb, :])
            P.dma('sp', g.vR[t0 + tb:t0 + tb + nb, :], o_[0:nb, :], csr())
            yield
            p_ = psr()
            P.mm(p_[0:nb, :], T["sgl"][:, tb:tb + nb], g2b[:, :])
            o_ = tkr()
            P.copy(ev2(), o_[0:nb, :], p_[0:nb, :])
            P.dma('sp', g.gR[t0 + tb:t0 + tb + nb, :], o_[0:nb, :], csr())

    def drive(gens):
        gens = list(gens)
        while gens:
            for gn in list(gens):
                try:
                    next(gn)
                except StopIteration:
                    gens.remove(gn)

    P.dma('sp', xt[0][:, :, 0:tiles[0][1]], g.xT.v(xTv[:, :, 0:tiles[0][1]]), cl[0])
    drive([stage1(0)])
    for ti in range(len(tiles)):
        gl = [stage2(ti)]
        if ti + 1 < len(tiles):
            gl.append(stage1(ti + 1))
        drive(gl)
    P.emit()


def phase_BR(nc, g, cfg, l):
    P = Prog(nc, "BR%d" % l)
    S = cfg.seq
    cc_ = P.chan("c")
    cl = [P.chan("l%d" % i) for i in range(2)]
    cs = [P.chan("s%d" % i) for i in range(3)]
    csr = Rot(cs)
    cst = load_consts(P, g, cc_)
    mk = {}
    for nm_, ci in (("I", 0), ("su", 5), ("sl", 6), ("iu", 3)):
        mk[nm_] = P.sbuf("mk_" + nm_, [64, NH, 64], F32)
        for h_ in range(NH):
            P.copy('pool', mk[nm_][:, h_, :], cst[0:64, ci, 0:64])
    rows = P.sbuf("rows", [64, 2, 512], F32)
    P.dma('sp', rows[:], V(g.prow.t[l:l + 1, 0:2, :].to_broadcast([64, 2, 512]), (g.prow.tok,)), cc_)

    GT = 128
    NG = 2
    streams = ("rt", "at", "bt", "kt", "cum")
    NSB = 3
    cl = [P.chan("l%d" % i) for i in range(NSB)]
    sb = [{n_: P.sbuf("%s%d" % (n_, i), [64, NH, GT], F32) for n_ in streams} for i in range(NSB)]
    vb = [P.sbuf("v%d" % i, [64, NG, 512], F32) for i in range(NSB)]
    gb = [P.sbuf("g%d" % i, [64, NG, 512], F32) for i in range(NSB)]
    bb = [P.sbuf("bn%d" % i, [64, NG, NH], F32) for i in range(NSB)]
    for i in range(NSB):
        for n_ in ("rt", "at", "bt", "kt"):
            sb[i][n_ + "b"] = P.sbuf("%sb%d" % (n_, i), [64, NH, GT], BF16)
    vbb = [P.sbuf("vb16_%d" % i, [64, NG, 512], BF16) for i in range(NSB)]
    MN = ("X", "X2", "AkT", "ArbT", "ArkT", "btok", "ktok")
    M = [{n_: [P.sbuf("m_%s%d_%d" % (n_, par, i), [64, NH, 64], F32 if n_ in ("X", "X2") else BF16)
               for i in range(NG)] for n_ in MN} for par in range(2)]
    Mtmp = {n_: [P.sbuf("m_%s_%d" % (n_, i), [64, NH, 64], F32) for i in range(NG)] for n_ in ("N", "A", "N2", "A2")}
    for par in range(2):
        M[par].update(Mtmp)
    GC = [[P.sbuf("gc%d_%d" % (par, i), [64, NH], F32) for i in range(NG)] for par in range(2)]
    ST = [P.sbuf("st%d" % i, [64, NH, 64], F32) for i in range(2)]
    Wsb = P.sbuf("wsb", [64, NH, 64], BF16)
    Xb = [[P.sbuf("xb%d_%d" % (par, i), [64, NH, 64], BF16) for i in range(NG)] for par in range(2)]
    Usb = [P.sbuf("usb%d" % i, [64, NH, 64], BF16) for i in range(2)]
    yt = [{n_: P.sbuf("y_%s%d" % (n_, i), [64, NH, 64], F32) for n_ in ("y", "o")} for i in range(2)]
    st1 = [{n_: P.sbuf("s_%s%d" % (n_, i), [64, NH], F32) for n_ in ("s1", "s2", "mu", "var", "rstd")} for i in range(2)]
    oT = [P.sbuf("oT%d" % i, [128, 4, 64], BF16) for i in range(2)]
    sio = P.sbuf("sio", [64, NH, 64], F32)
    ps = [P.psum("ps%d" % i, [128, 512]) for i in range(8)]
    psr = Rot(ps[0:5])
    psq = Rot(ps[5:8])
    evr = Rot(['act', 'dve'])

    def ps3(p_, C1, C2):
        return p_.v(p_.t[0:C1, :].rearrange("p (h c) -> p h c", h=NH)[:, :, 0:C2])

    def psc(p_, C):
        return p_.v(p_.t[0:C, 0:NH * C].rearrange("p (h c) -> p h c", h=NH))

    groups = [(i * GT, 64, 2, 'p', 0) for i in range(S // GT)] + [(S, LS, 2, 's', 0), (S + 2 * LS, LS, 2, 's', 2)]
    nprompt_groups = S // GT
    state = {"stcur": 0, "ycnt": 0}
    P.memset('pool', ST[0][:], 0.0)
    finalX = {}

    def load_group(gi):
        t0, C, nch, kind, sq0 = groups[gi]
        b = gi % NSB
        n = C * nch
        for n_ in streams:
            src = getattr(g, n_)
            P.dma('sp', sb[b][n_][:, :, 0:n], src.v(src.t.rearrange("(h j) t -> j h t", j=64)[:, :, t0:t0 + n]), cl[b])
        P.dma('sp', vb[b][0:C, 0:nch, :], g.vR.v(g.vR.t[t0:t0 + n, :].rearrange("(g t) c -> t g c", t=C)), cl[b])
        P.dma('sp', gb[b][0:C, 0:nch, :], g.gR.v(g.gR.t[t0:t0 + n, :].rearrange("(g t) c -> t g c", t=C)), cl[b])
        P.dma('sp', bb[b][0:C, 0:nch, :], g.bnR.v(g.bnR.t[t0:t0 + n, :].rearrange("(g t) c -> t g c", t=C)), cl[b])

    def indep(gi):
        t0, C, nch, kind, sq0 = groups[gi]
        b = gi % NSB
        s_ = sb[b]
        Mg = M[gi % 2]
        GCg = GC[gi % 2]
        nsteps = {64: 5, 16: 3}[C]

        def cs_(n_, ci, h_):
            return s_[n_][:, h_, ci * C:(ci + 1) * C]

        def stage(outname, lname, rname, mask):
            for ci in range(nch):
                p_ = psr()
                for h_ in range(NH):
                    P.mm(p_[0:C, h_ * C:(h_ + 1) * C], cs_(lname, ci, h_), cs_(rname, ci, h_))
                P.tt('dve', Mg[outname][ci][0:C, :, 0:C], psc(p_, C), mk[mask][0:C, :, 0:C], ALU.mult)

        n_ = C * nch
        for nm_ in ("rt", "at", "bt", "kt"):
            P.copy('pool', s_[nm_ + "b"][:, :, 0:n_], s_[nm_][:, :, 0:n_])
        P.copy('pool', vbb[b][0:C, 0:nch, :], vb[b][0:C, 0:nch, :])
        stage("N", "bt", "at", "su")
        yield
        stage("A", "at", "bt", "sl")
        yield
        for ci in range(nch):
            P.tt('pool', Mg["X"][ci][0:C, :, 0:C], Mg["N"][ci][0:C, :, 0:C], mk["I"][0:C, :, 0:C], ALU.add)
            P.act(GCg[ci][:, :], s_["cum"][:, :, (ci + 1) * C - 1], AF.Exp, scale=LWC)
        nN, nA, nN2, nA2, nX, nX2 = "N", "A", "N2", "A2", "X", "X2"
        extra = [("AkT", "ktb", "atb", "su"), ("ArbT", "btb", "rtb", "iu"), ("ArkT", "ktb", "rtb", "iu")]
        trs = [("btok", "bt"), ("ktok", "kt")]
        for it in range(nsteps):
            last = it == nsteps - 1
            for ci in range(nch):
                if not last:
                    p_ = psr()
                    for h_ in range(NH):
                        P.mm(p_[0:C, h_ * C:(h_ + 1) * C], Mg[nA][ci][0:C, h_, 0:C], Mg[nN][ci][0:C, h_, 0:C])
                    P.copy('act', Mg[nN2][ci][0:C, :, 0:C], psc(p_, C))
                p_ = psr()
                for h_ in range(NH):
                    P.mm(p_[0:C, h_ * C:(h_ + 1) * C], Mg[nN][ci][0:C, h_, 0:C], Mg[nA][ci][0:C, h_, 0:C])
                P.copy('act', Mg[nA2][ci][0:C, :, 0:C], psc(p_, C))
            if extra:
                stage(*extra.pop(0))
            elif trs:
                oname, sname = trs.pop(0)
                for ci in range(nch):
                    p_ = psr()
                    for h_ in range(NH):
                        P.tr(p_[0:C, h_ * 64:(h_ + 1) * 64], cs_(sname, ci, h_), cst[0:64, 0, 0:64])
                    P.copy('act', Mg[oname][ci][0:C, :, :], ps3(p_, C, 64))
            yield
            for ci in range(nch):
                p_ = psr()
                for h_ in range(NH):
                    P.mm(p_[0:C, h_ * C:(h_ + 1) * C], Mg[nA2][ci][0:C, h_, 0:C], Mg[nX][ci][0:C, h_, 0:C])
                P.tt('dve', Mg[nX2][ci][0:C, :, 0:C], psc(p_, C), Mg[nX][ci][0:C, :, 0:C], ALU.add)
            yield
            nN, nN2 = nN2, nN
            nA, nA2 = nA2, nA
            nX, nX2 = nX2, nX
        while extra:
            stage(*extra.pop(0))
            yield
        while trs:
            oname, sname = trs.pop(0)
            for ci in range(nch):
                p_ = psr()
                for h_ in range(NH):
                    P.tr(p_[0:C, h_ * 64:(h_ + 1) * 64], cs_(sname, ci, h_), cst[0:64, 0, 0:64])
                P.copy('act', Mg[oname][ci][0:C, :, :], ps3(p_, C, 64))
            yield
        finalX[gi] = nX
        for ci in range(nch):
            P.copy('pool', Xb[gi % 2][ci][0:C, :, 0:C], Mg[nX][ci][0:C, :, 0:C])

    def seq(gi):
        t0, C, nch, kind, sq0 = groups[gi]
        b = gi % NSB
        s_ = sb[b]
        Mg = M[gi % 2]
        GCg = GC[gi % 2]

        def cs_(n_, ci, h_):
            return s_[n_][:, h_, ci * C:(ci + 1) * C]

        for ci in range(nch):
            tc0 = t0 + ci * C
            if kind == 's':
                P.dma('sp', sio[:], V(g.state_in.t[l, sq0 + ci].rearrange("h i j -> i h j"), (g.state_in.tok,)), cc_)
                p_ = psq()
                for h_ in range(NH):
                    P.tr(p_[0:64, h_ * 64:(h_ + 1) * 64], sio[:, h_, :], cst[0:64, 0, 0:64])
                state["stcur"] = 0
                P.copy('dve', ST[0][:], ps3(p_, 64, 64))
            So = ST[state["stcur"]]
            Sn = ST[1 - state["stcur"]]
            X = Xb[gi % 2][ci]
            Vc = vb[b]
            Vm = vbb[b]
            p_ = psq()
            for h_ in range(NH):
                P.mm(p_[0:C, h_ * 64:(h_ + 1) * 64], cs_("at", ci, h_), So[:, h_, :], start=True, stop=False)
                P.mm(p_[0:C, h_ * 64:(h_ + 1) * 64], Mg["AkT"][ci][0:C, h_, 0:C], Vm[0:C, ci, h_ * 64:(h_ + 1) * 64],
                     start=False, stop=True)
            P.copy('act', Wsb[0:C, :, :], ps3(p_, C, 64))
            yield
            p_ = psq()
            for h_ in range(NH):
                P.mm(p_[0:C, h_ * 64:(h_ + 1) * 64], X[0:C, h_, 0:C], Wsb[0:C, h_, :])
            U = Usb[state["ycnt"] % 2]
            P.copy('dve', U[0:C, :, :], ps3(p_, C, 64))
            yield
            p_ = psq()
            for h_ in range(NH):
                o_ = p_[0:64, h_ * 64:(h_ + 1) * 64]
                P.mm(o_, cst[0:64, 0, 0:64], So[:, h_, :], start=True, stop=False)
                P.mm(o_, Mg["btok"][ci][0:C, h_, :], U[0:C, h_, :], start=False, stop=False)
                P.mm(o_, Mg["ktok"][ci][0:C, h_, :], Vm[0:C, ci, h_ * 64:(h_ + 1) * 64], start=False, stop=True)
            P.tt('dve', Sn[:], ps3(p_, 64, 64), GCg[ci].v(GCg[ci].t[:, :, None].to_broadcast([64, NH, 64])), ALU.mult)
            py = psq()
            for h_ in range(NH):
                o_ = py[0:C, h_ * 64:(h_ + 1) * 64]
                P.mm(o_, cs_("rt", ci, h_), So[:, h_, :], start=True, stop=False)
                P.mm(o_, Mg["ArbT"][ci][0:C, h_, 0:C], U[0:C, h_, :], start=False, stop=False)
                P.mm(o_, Mg["ArkT"][ci][0:C, h_, 0:C], Vm[0:C, ci, h_ * 64:(h_ + 1) * 64], start=False, stop=True)
            state["stcur"] = 1 - state["stcur"]
            yb = yt[state["ycnt"] % 2]
            sb_ = st1[state["ycnt"] % 2]
            state["ycnt"] += 1
            y = yb["y"]
            P.copy('act', y[0:C], ps3(py, C, 64))
            yield
            P.rsum(sb_["s1"][0:C, :], y[0:C])
            P.tt('pool', yb["o"][0:C], y[0:C], y[0:C], ALU.mult)
            P.rsum(sb_["s2"][0:C, :], yb["o"][0:C])
            P.ts('dve', sb_["mu"][0:C, :], sb_["s1"][0:C, :], 1.0 / 64, ALU.mult)
            P.tt('dve', sb_["var"][0:C, :], sb_["mu"][0:C, :], sb_["mu"][0:C, :], ALU.mult)
            P.stt('dve', sb_["var"][0:C, :], sb_["s2"][0:C, :], 1.0 / 64, sb_["var"][0:C, :], ALU.mult, ALU.subtract)
            P.act(sb_["rstd"][0:C, :], sb_["var"][0:C, :], AF.Sqrt, bias=GN_EPS, scale=1.0)
            P.recip(sb_["rstd"][0:C, :], sb_["rstd"][0:C, :])

            def bc(bf_):
                return bf_.v(bf_.t[0:C, :, None].to_broadcast([C, NH, 64]))
            P.tt('pool', y[0:C], y[0:C], bc(sb_["mu"]), ALU.subtract)
            P.tt('pool', y[0:C], y[0:C], bc(sb_["rstd"]), ALU.mult)
            yield
            r3 = rows.t[0:C].rearrange("p a (h c) -> p a h c", h=NH)
            P.tt('pool', y[0:C], y[0:C], rows.v(r3[:, 0]), ALU.mult)
            P.tt('pool', y[0:C], y[0:C], rows.v(r3[:, 1]), ALU.add)
            v3 = Vc.v(Vc.t[0:C, ci, :].rearrange("p (h c) -> p h c", h=NH))
            g3 = gb[b].v(gb[b].t[0:C, ci, :].rearrange("p (h c) -> p h c", h=NH))
            bn3 = bb[b].v(bb[b].t[0:C, ci, :, None].to_broadcast([C, NH, 64]))
            P.tt('pool', yb["o"][0:C], v3, bn3, ALU.mult)
            P.tt('pool', y[0:C], y[0:C], yb["o"][0:C], ALU.add)
            P.tt('pool', yb["o"][0:C], y[0:C], g3, ALU.mult)
            p_ = psq()
            ov = yb["o"].t[0:C].rearrange("p h c -> p (h c)")
            for j in range(4):
                P.tr(p_[:, j * 64:j * 64 + C], yb["o"].v(ov[:, j * 128:(j + 1) * 128]), cst[0:C, 0, 0:C])
            ot = oT[state["ycnt"] % 2]
            P.copy('act', ot[:, :, 0:C], p_.v(p_.t[:, 0:256].rearrange("p (j c) -> p j c", j=4)[:, :, 0:C]))
            P.dma('sp', g.oaT.v(g.oaT.t.rearrange("(c p) t -> p c t", p=128)[:, :, tc0:tc0 + C]), ot[:, :, 0:C], csr())
            if kind == 's' or (gi == nprompt_groups - 1 and ci == nch - 1):
                Sf = ST[state["stcur"]]
                p_ = psq()
                for h_ in range(NH):
                    P.tr(p_[0:64, h_ * 64:(h_ + 1) * 64], Sf[:, h_, :], cst[0:64, 0, 0:64])
                P.copy('dve', sio[:], ps3(p_, 64, 64))
                dst = g.sd.t[l, sq0 + ci] if kind == 's' else g.sp_.t[l]
                dtok = g.sd if kind == 's' else g.sp_
                P.dma('sp', dtok.v(dst.rearrange("h i j -> i h j")), sio[:], cc_)
            yield

    def drive(gens):
        gens = list(gens)
        while gens:
            for gn in list(gens):
                try:
                    next(gn)
                except StopIteration:
                    gens.remove(gn)

    load_group(0)
    if len(groups) > 1:
        load_group(1)
    drive([indep(0)])
    for gi in range(len(groups)):
        if gi + 2 < len(groups):
            load_group(gi + 2)
        gl = [seq(gi)]
        if gi + 1 < len(groups):
            gl.append(indep(gi + 1))
        drive(gl)
    P.emit()


def phase_BF(nc, g, cfg, l):
    P = Prog(nc, "BF%d" % l)
    S = cfg.seq
    PAST = cfg.past
    NKB = S // 128
    NPB = PAST // 128
    cc_ = P.chan("c")
    cl = [P.chan("l%d" % i) for i in range(3)]
    cs = [P.chan("s%d" % i) for i in range(2)]
    csr = Rot(cs)
    cst = load_consts(P, g, cc_)
    onesb = P.sbuf("onesb", [128, 128], BF16)
    P.memset('pool', onesb[:], 1.0)
    qz = {0: [P.sbuf("qzE%d" % i, [128, 512], BF16) for i in range(2)],
          1: [P.sbuf("qzO%d" % i, [128, 512], BF16) for i in range(2)]}
    for par in (0, 1):
        for b_ in qz[par]:
            P.memset('pool', b_[:], 0.0)
    qzc = [0]
    trib = P.sbuf("trib", [128, 128], BF16)
    P.copy('pool', trib[:], cst[:, 3, :])
    qT = P.sbuf("qT", [128, 4, S], BF16)
    kT = P.sbuf("kT", [128, 4, S], BF16)
    vF = P.sbuf("vF", [128, NKB, 512], BF16)
    P.dma('sp', qT[:], g.qT.v(g.qT.t.rearrange("(c p) t -> p c t", p=128)[:, :, 0:S]), cl[0])
    P.dma('sp', kT[:], g.kT.v(g.kT.t.rearrange("(c p) t -> p c t", p=128)[:, :, 0:S]), cl[1])
    P.dma('sp', vF[:], g.vF.v(g.vF.t[0:S, :].rearrange("(b p) c -> p b c", p=128)), cl[2])
    lf = P.sbuf("lf", [128, NKB, NH], F32)
    P.dma('sp', lf[:], g.lf.v(g.lf.t[0:S, :].rearrange("(b p) c -> p b c", p=128)), cc_)
    cum = P.sbuf("cum", [128, NKB, NH], F32)
    negc = P.sbuf("negc", [128, NKB, NH], F32)
    Rb = P.sbuf("Rb", [128, NH], F32)
    biasq = [P.sbuf("biasq%d" % i, [128, NKB, NH], F32) for i in range(2)]
    biass = P.sbuf("biass", [128, NPB + 1, NH], F32)
    pT = [P.sbuf("pT%d" % i, [128, 512], BF16) for i in range(8)]
    pTr = Rot(pT)
    rec = [P.sbuf("rec%d" % i, [128, 512], F32) for i in range(2)]
    ob = [P.sbuf("ob%d" % i, [128, 512], BF16) for i in range(2)]
    ps = [P.psum("ps%d" % i, [128, 512]) for i in range(8)]
    pss = Rot(ps[0:3])
    pso = Rot(ps[3:5])
    psd = Rot(ps[5:7])
    psm = ps[7]

    def cumsum_blocks(lfv, nblk, nlast, cumv, carry_view):
        for b_ in range(nblk):
            n_ = 128 if b_ < nblk - 1 else nlast
            first = (b_ == 0 and carry_view is None)
            P.mm(psm[0:n_, 0:NH], cst[0:n_, 3, 0:n_], lfv(b_, n_), start=True, stop=first)
            if not first:
                prev = carry_view if b_ == 0 else cumv(b_ - 1, 128)
                P.mm(psm[0:n_, 0:NH], cst[:, 4, 0:n_], prev, start=False, stop=True)
            P.copy('dve', cumv(b_, n_), psm[0:n_, 0:NH])

    cumsum_blocks(lambda b_, n_: lf[0:n_, b_, :], NKB, 128, lambda b_, n_: cum[0:n_, b_, :], None)
    P.ts('dve', negc[:], cum[:], -1.0, ALU.mult)
    obv = g.obT.t.rearrange("(c p) t -> p c t", p=128)
    sel = P.sbuf("sel", [128, NH, 128], BF16)
    P.memset('pool', sel[:], 0.0)
    P.copy('pool', sel[0:NH], cst.v(cst.t[0:NH, 0, 0:NH, None].to_broadcast([NH, NH, 128])))
    cumT = P.sbuf("cumT", [NH, S], F32)
    cqs = P.sbuf("cqs", [128, S], BF16)
    P.memset('pool', cqs[:], 0.0)
    for k4 in range(0, NKB, 4):
        for kb in range(k4, min(NKB, k4 + 4)):
            P.tr(psm[0:NH, (kb - k4) * 128:(kb - k4 + 1) * 128], cum[:, kb, :], cst[:, 0, :])
        nb4 = min(NKB, k4 + 4) - k4
        P.copy('dve', cumT[:, k4 * 128:(k4 + nb4) * 128], psm[0:NH, 0:nb4 * 128])

    def attend(h_, qv, nq, blocks, out_rows, cqv):
        base = 64 * (h_ % 2)
        po = pso()
        pd = psd()
        qzb = qz[h_ % 2][qzc[0] % 2]
        if h_ % 2 == 1:
            qzc[0] += 1
        P.copy('pool', qzb[base:base + 64, 0:nq], qv(0, nq))
        DEPTH = 3
        pts = {}
        nb_ = len(blocks)
        for bi in range(nb_ + DEPTH):
            if bi < nb_:
                kv, vv, bv, nk, q0, masked = blocks[bi]
                n = nq - q0
                p_ = pss()
                P.mm(p_[0:nk, 0:n], kv, qzb[:, q0:nq], start=True, stop=(cqv is None))
                if cqv is not None:
                    P.mm(p_[0:nk, 0:n], sel[:, h_, 0:nk], cqv(q0, nq), start=False, stop=True)
                pt = pTr()
                P.act(pt[0:nk, 0:n], p_[0:nk, 0:n], AF.Exp, bias=bv[:, h_:h_ + 1], scale=SCALE)
                if masked:
                    m_ = min(nk, n)
                    P.tt('pool', pt[0:nk, 0:m_], pt[0:nk, 0:m_], trib[0:nk, 0:m_], ALU.mult)
                pts[bi] = pt
            bj = bi - DEPTH
            if bj >= 0:
                kv, vv, bv, nk, q0, masked = blocks[bj]
                n = nq - q0
                pt = pts.pop(bj)
                P.mm(po[:, q0:nq], vv, pt[0:nk, 0:n], start=(bj == 0), stop=(bj == nb_ - 1))
                P.mm(pd[:, q0:nq], onesb[0:nk, :], pt[0:nk, 0:n], start=(bj == 0), stop=(bj == nb_ - 1))
        rc = rec[h_ % 2]
        P.recip(rc[base:base + 64, 0:nq], pd[base:base + 64, 0:nq])
        P.tt('dve', out_rows[base:base + 64, 0:nq], po[base:base + 64, 0:nq], rc[base:base + 64, 0:nq], ALU.mult)

    NQ = 512
    for qi in range(S // NQ):
        lastb = (qi + 1) * 4 - 1
        P.mm(psm[:, 0:NH], cst[:, 4, :], cum[:, lastb, :])
        P.copy('dve', Rb[:], psm[:, 0:NH])
        qe = (qi + 1) * NQ
        bq = biasq[qi % 2]
        P.tt('dve', bq[:, 0:lastb + 1, :], negc[:, 0:lastb + 1, :],
             Rb.v(Rb.t[:, None, :].to_broadcast([128, lastb + 1, NH])), ALU.add)
        P.ts('dve', cqs[0:NH, qi * NQ:qe], cumT[:, qi * NQ:qe], cumT[:, qe - 1:qe], ALU.subtract, 1.0 / SCALE, ALU.mult)
        for p2 in range(4):
            o_ = ob[p2 % 2]
            for h_ in (2 * p2, 2 * p2 + 1):
                base = 64 * (h_ % 2)
                blocks = []
                for kb in range(lastb + 1):
                    m_ = kb - 4 * qi
                    q0 = max(0, m_) * 128
                    blocks.append((kT[:, p2, kb * 128:(kb + 1) * 128],
                                   vF[:, kb, p2 * 128:(p2 + 1) * 128], bq[:, kb, :], 128, q0, m_ >= 0))
                attend(h_, lambda q0, nq, base=base, p2=p2, qi=qi: qT[base:base + 64, p2, qi * NQ + q0:qi * NQ + nq],
                       NQ, blocks, o_, lambda q0, nq, qi=qi: cqs[:, qi * NQ + q0:qi * NQ + nq])
            P.dma('sp', g.obT.v(obv[:, p2, qi * NQ:(qi + 1) * NQ]), o_[:, 0:NQ], csr())

    kc = P.sbuf("kc", [128, NPB, 512], F32)
    kTc = P.sbuf("kTc", [128, 4, PAST], BF16)
    vc = P.sbuf("vc", [128, NPB, 512], BF16)
    lfc = P.sbuf("lfc", [128, NPB + 1, NH], F32)
    cumc = P.sbuf("cumc", [128, NPB + 1, NH], F32)
    negcc = P.sbuf("negcc", [128, NPB + 1, NH], F32)
    qTs = P.sbuf("qTs", [128, 4, NS * LS], BF16)
    kTs = P.sbuf("kTs", [128, 4, NS * LS], BF16)
    vFs = P.sbuf("vFs", [LS, NS, 512], BF16)
    obs = P.sbuf("obs", [128, 4, NS * LS], BF16)
    P.dma('sp', qTs[:], g.qT.v(g.qT.t.rearrange("(c p) t -> p c t", p=128)[:, :, S:S + NS * LS]), cc_)
    P.dma('sp', kTs[:], g.kT.v(g.kT.t.rearrange("(c p) t -> p c t", p=128)[:, :, S:S + NS * LS]), cc_)
    P.dma('sp', vFs[:], g.vF.v(g.vF.t[S:S + NS * LS, :].rearrange("(s t) c -> t s c", t=LS)), cc_)
    cck = P.chan("ck")
    ccv = P.chan("cv")
    evr = Rot(['act', 'dve'])
    for s_ in range(NS):
        P.dma('sp', kc[:], V(g.cache_k.t[l, s_].rearrange("(b p) c -> p b c", p=128), (g.cache_k.tok,)), cck)
        P.dma('pool', vc[:], V(g.cache_v.t[l, s_].rearrange("(b p) c -> p b c", p=128), (g.cache_v.tok,)), ccv)
        P.dma('sp', lfc[:, 0:NPB, :], V(g.cache_lf.t[l, s_].rearrange("(b p) c -> p b c", p=128), (g.cache_lf.tok,)), cck)
        P.dma('sp', lfc[0:LS, NPB, :], g.lf[S + s_ * LS:S + (s_ + 1) * LS, :], cck)
        for kb in range(NPB):
            p_ = pss()
            for p2 in range(4):
                P.tr(p_[:, p2 * 128:(p2 + 1) * 128], kc[:, kb, p2 * 128:(p2 + 1) * 128], cst[:, 0, :])
            P.copy(evr(), kTc[:, :, kb * 128:(kb + 1) * 128], p_.v(p_.t[:, :].rearrange("p (j c) -> p j c", j=4)))
        cumsum_blocks(lambda b_, n_: lfc[0:n_, b_, :], NPB + 1, LS, lambda b_, n_: cumc[0:n_, b_, :], None)
        P.ts('dve', negcc[:], cumc[:], -1.0, ALU.mult)
        P.mm(psm[:, 0:NH], cst[0:LS, 8, :], cumc[0:LS, NPB, :])
        P.copy('dve', Rb[:], psm[:, 0:NH])
        P.tt('dve', biass[:], negcc[:], Rb.v(Rb.t[:, None, :].to_broadcast([128, NPB + 1, NH])), ALU.add)
        for h_ in range(NH):
            p2 = h_ // 2
            base = 64 * (h_ % 2)
            blocks = []
            for kb in range(NPB):
                blocks.append((kTc[:, p2, kb * 128:(kb + 1) * 128], vc[:, kb, p2 * 128:(p2 + 1) * 128],
                               biass[:, kb, :], 128, 0, False))
            blocks.append((kTs[:, p2, s_ * LS:(s_ + 1) * LS], vFs[0:LS, s_, p2 * 128:(p2 + 1) * 128],
                           biass[0:LS, NPB, :], LS, 0, True))
            attend(h_, lambda q0, nq, base=base, p2=p2, s_=s_: qTs[base:base + 64, p2, s_ * LS + q0:s_ * LS + nq],
                   LS, blocks, obs.v(obs.t[:, p2, s_ * LS:(s_ + 1) * LS]), None)
    P.dma('sp', g.obT.v(obv[:, :, S:S + NS * LS]), obs[:], csr())
    P.emit()


def phase_C1(nc, g, cfg, l):
    P = Prog(nc, "C1%d" % l)
    cw = P.chan("w")
    cl = [P.chan("l%d" % i) for i in range(2)]
    cs = [P.chan("s%d" % i) for i in range(2)]
    pa = P.sbuf("pa", [128, 4, D], BF16)
    pb = P.sbuf("pb", [128, 4, D], BF16)
    wo = P.sbuf("wo", [128, 8, D], BF16)
    load_w_bf16(P, pa, g.p_a.t[l], 4, D, cw)
    load_w_bf16(P, pb, g.p_b.t[l], 4, D, cw)
    load_w_bf16(P, wo, g.w_out.t[l], 8, D, cw)
    xt = [P.sbuf("xt%d" % i, [128, 8, 512], F32) for i in range(2)]
    oa = [P.sbuf("oa%d" % i, [128, 4, 512], BF16) for i in range(2)]
    ob = [P.sbuf("ob%d" % i, [128, 4, 512], BF16) for i in range(2)]
    ga = [P.sbuf("ga%d" % i, [128, 8, 512], BF16) for i in range(2)]
    gb = [P.sbuf("gb%d" % i, [128, 8, 512], BF16) for i in range(2)]
    ma = [P.sbuf("ma%d" % i, [128, 512], F32) for i in range(2)]
    mar = Rot(ma)
    mT = P.sbuf("mT", [128, 8, 512], BF16)
    ps = [P.psum("ps%d" % i, [128, 512]) for i in range(8)]
    psr = Rot(ps)
    v3 = lambda b_: b_.t.rearrange("(c p) t -> p c t", p=128)
    tiles = cfg.tiles

    def load(ti):
        t0, nt = tiles[ti][0], tiles[ti][1]
        b = ti % 2
        P.dma('sp', xt[b][:, :, 0:nt], g.xT.v(v3(g.xT)[:, :, t0:t0 + nt]), cl[b])
        P.dma('sp', oa[b][:, :, 0:nt], g.oaT.v(v3(g.oaT)[:, :, t0:t0 + nt]), cl[b])
        P.dma('sp', ob[b][:, :, 0:nt], g.obT.v(v3(g.obT)[:, :, t0:t0 + nt]), cl[b])
        P.dma('sp', ga[b][:, :, 0:nt], g.gaT.v(v3(g.gaT)[:, :, t0:t0 + nt]), cl[b])
        P.dma('sp', gb[b][:, :, 0:nt], g.gbT.v(v3(g.gbT)[:, :, t0:t0 + nt]), cl[b])

    load(0)
    for ti, (t0, nt, nseq, L, C) in enumerate(tiles):
        if ti + 1 < len(tiles):
            load(ti + 1)
        b = ti % 2
        for c in range(8):
            p1 = psr()
            for k in range(4):
                P.mm(p1[:, 0:nt], pa[:, k, c * 128:(c + 1) * 128], oa[b][:, k, 0:nt], start=(k == 0), stop=(k == 3))
            m_ = mar()
            P.tt('dve', m_[:, 0:nt], p1[:, 0:nt], ga[b][:, c, 0:nt], ALU.mult)
            p2 = psr()
            for k in range(4):
                P.mm(p2[:, 0:nt], pb[:, k, c * 128:(c + 1) * 128], ob[b][:, k, 0:nt], start=(k == 0), stop=(k == 3))
            P.tt('dve', mT[:, c, 0:nt], p2[:, 0:nt], gb[b][:, c, 0:nt], ALU.mult)
            P.tt('pool', mT[:, c, 0:nt], mT[:, c, 0:nt], m_[:, 0:nt], ALU.add)
        for c in range(8):
            p1 = psr()
            for k in range(8):
                P.mm(p1[:, 0:nt], wo[:, k, c * 128:(c + 1) * 128], mT[:, k, 0:nt], start=(k == 0), stop=(k == 7))
            P.tt('dve', xt[b][:, c, 0:nt], xt[b][:, c, 0:nt], p1[:, 0:nt], ALU.add)
        P.dma('sp', g.xT.v(v3(g.xT)[:, :, t0:t0 + nt]), xt[b][:, :, 0:nt], cs[b])
    P.emit()


def phase_C2(nc, g, cfg, l):
    P = Prog(nc, "C2%d" % l)
    cw = P.chan("w")
    cc_ = P.chan("c")
    cl = [P.chan("l%d" % i) for i in range(2)]
    cs = [P.chan("s%d" % i) for i in range(2)]
    cst = load_consts(P, g, cc_)
    pv = P.sbuf("pv", [128, PV_N], F32)
    P.dma('sp', pv[:], g.pvec[l], cc_)
    wg = WBuf(P, "wg", 8, DFF, 512)
    wu = WBuf(P, "wu", 8, DFF, 512)
    wd = P.sbuf("wd", [128, NFF, D], BF16)
    chs = [P.chan("w%d" % i) for i in range(8)]
    i_ = 0
    for c0 in range(0, DFF, 512):
        c1 = min(DFF, c0 + 512)
        for (dst_, src_) in ((wg, g.w_gate), (wu, g.w_up)):
            for k in range(8):
                P.dma('pool', dst_[:, k, c0:c1], V(src_.t[l][k * 128:(k + 1) * 128, c0:c1], ()), chs[i_ % 8])
                i_ += 1
    load_w_bf16(P, wd, g.w_down.t[l], NFF, D, cw)
    NT = 256
    xt = [P.sbuf("xt%d" % i, [128, 8, NT], F32) for i in range(2)]
    sqb = [P.sbuf("sq%d" % i, [128, NT], F32) for i in range(2)]
    sqr = Rot(sqb)
    rs = P.sbuf("rs", [128, NT], F32)
    hT = P.sbuf("hT", [128, 8, NT], BF16)
    sl = [P.sbuf("sl%d" % i, [128, NT], F32) for i in range(2)]
    slr = Rot(sl)
    hid = P.sbuf("hid", [128, NFF, NT], BF16)
    ps = [P.psum("ps%d" % i, [128, 512]) for i in range(8)]
    psr = Rot(ps[0:7])
    ps_ss = ps[7]
    evr = Rot(['dve', 'pool'])
    v3 = g.xT.t.rearrange("(c p) t -> p c t", p=128)
    tiles = []
    for (t0, nt, _, _, _) in cfg.tiles:
        for o in range(0, nt, NT):
            tiles.append((t0 + o, min(NT, nt - o)))
    P.dma('sp', xt[0][:, :, 0:tiles[0][1]], g.xT.v(v3[:, :, tiles[0][0]:tiles[0][0] + tiles[0][1]]), cl[0])
    for ti, (t0, nt) in enumerate(tiles):
        if ti + 1 < len(tiles):
            t0n, ntn = tiles[ti + 1]
            P.dma('sp', xt[(ti + 1) % 2][:, :, 0:ntn], g.xT.v(v3[:, :, t0n:t0n + ntn]), cl[(ti + 1) % 2])
        x = xt[ti % 2]
        rmsnorm_tile(P, x, nt, PV_G2, pv, cst, ps_ss, sqr, rs, hT, evr)
        for f in range(NFF):
            pg = psr()
            pu = psr()
            for k in range(8):
                P.mm(pg[:, 0:nt], wg[:, k, f * 128:(f + 1) * 128], hT[:, k, 0:nt], start=(k == 0), stop=(k == 7))
            for k in range(8):
                P.mm(pu[:, 0:nt], wu[:, k, f * 128:(f + 1) * 128], hT[:, k, 0:nt], start=(k == 0), stop=(k == 7))
            s_ = slr()
            P.act(s_[:, 0:nt], pg[:, 0:nt], AF.Silu)
            P.tt('dve', hid[:, f, 0:nt], pu[:, 0:nt], s_[:, 0:nt], ALU.mult)
        for c in range(8):
            p1 = psr()
            for f in range(NFF):
                P.mm(p1[:, 0:nt], wd[:, f, c * 128:(c + 1) * 128], hid[:, f, 0:nt], start=(f == 0), stop=(f == NFF - 1))
            P.tt('dve', x[:, c, 0:nt], x[:, c, 0:nt], p1[:, 0:nt], ALU.add)
        P.dma('sp', g.xT.v(v3[:, :, t0:t0 + nt]), x[:, :, 0:nt], cs[ti % 2])
    P.emit()


def phase_F(nc, g, cfg):
    P = Prog(nc, "F")
    cc_ = P.chan("c")
    cl = [P.chan("l%d" % i) for i in range(2)]
    cs = [P.chan("s%d" % i) for i in range(2)]
    cst = load_consts(P, g, cc_)
    pv = P.sbuf("pv", [128, 8], F32)
    P.dma('sp', pv[:], g.fing[:], cc_)
    xt = [P.sbuf("xt%d" % i, [128, 8, 128], F32) for i in range(2)]
    sqb = [P.sbuf("sq%d" % i, [128, 128], F32) for i in range(2)]
    sqr = Rot(sqb)
    rs = P.sbuf("rs", [128, 128], F32)
    hT = P.sbuf("hT", [128, 8, 128], F32)
    yo = [P.sbuf("yo%d" % i, [128, D], F32) for i in range(2)]
    ps = [P.psum("ps%d" % i, [128, 512]) for i in range(5)]
    psr = Rot(ps[0:4])
    ps_ss = ps[4]
    evr = Rot(['dve', 'pool'])
    ev2 = Rot(['act', 'dve'])
    v3 = g.xT.t.rearrange("(c p) t -> p c t", p=128)
    blocks = [(g.y_prompt, i * 128, 128, i * 128) for i in range(cfg.seq // 128)] + [(g.y_sample, 0, NS * LS, cfg.seq)]
    P.dma('sp', xt[0][:, :, 0:blocks[0][2]], g.xT.v(v3[:, :, 0:blocks[0][2]]), cl[0])
    for bi, (dst, r0, nr, t0) in enumerate(blocks):
        if bi + 1 < len(blocks):
            _, _, nrn, t0n = blocks[bi + 1]
            P.dma('sp', xt[(bi + 1) % 2][:, :, 0:nrn], g.xT.v(v3[:, :, t0n:t0n + nrn]), cl[(bi + 1) % 2])
        x = xt[bi % 2]
        rmsnorm_tile(P, x, nr, 0, pv, cst, ps_ss, sqr, rs, hT, evr)
        y = yo[bi % 2]
        for hh in range(2):
            p_ = psr()
            for c in range(4):
                P.tr(p_[0:nr, c * 128:(c + 1) * 128], hT[:, hh * 4 + c, 0:nr], cst[:, 0, :])
            P.copy(ev2(), y[0:nr, hh * 512:(hh + 1) * 512], p_[0:nr, :])
        P.dma('sp', dst[r0:r0 + nr, :], y[0:nr, :], cs[bi % 2])
    P.emit()


def build(cfg, phases=None):
    nc = bass.Bass("TRN2", target_bir_lowering=False)
    nc._sempool = SemPool(nc)
    g = declare(nc, cfg)
    ph = getattr(cfg, 'phases', None) or ['T0', 'A', 'BR', 'BF', 'C1', 'C2', 'F']
    if 'T0' in ph:
        phase_T0(nc, g, cfg)
    for l in range(cfg.depth):
        if 'A' in ph:
            phase_A(nc, g, cfg, l)
        if 'BR' in ph:
            phase_BR(nc, g, cfg, l)
        if 'BF' in ph:
            phase_BF(nc, g, cfg, l)
        if 'C1' in ph:
            phase_C1(nc, g, cfg, l)
        if 'C2' in ph:
            phase_C2(nc, g, cfg, l)
    if 'F' in ph:
        phase_F(nc, g, cfg)
    nc._sempool.stack.close()
    return nc


def make_consts():
    c = np.zeros((128, 9, 128), np.float32)
    i = np.arange(128)
    c[:, 0, :] = np.eye(128)
    c[:, 1, :] = 1.0
    c[:, 2, :] = (i[:, None] // 64 == i[None, :] // 64)
    c[:, 3, :] = (i[:, None] <= i[None, :])
    c[127, 4, :] = 1.0
    c[:, 5, :] = (i[:, None] < i[None, :])
    c[:, 6, :] = (i[:, None] > i[None, :])
    c[:, 7, :] = (i[:, None] <= i[None, :])
    c[LS - 1, 8, :] = 1.0
    hs = np.zeros((128, 2), np.float32)
    hs[0:64, 0] = 1.0
    hs[64:128, 1] = 1.0
    return c, hs


def col(v, n):
    return np.ascontiguousarray(np.asarray(v, np.float32).reshape(n, 128).T)


def kernel(x_prompt, x_sample, cache_fox_k, cache_fox_v, cache_fox_logf, state_rwkv, state_shift,
           norm1_g, w_in, rwkv_mu, rwkv_w0, rwkv_w2, rwkv_a0, rwkv_a2, rwkv_g2, rwkv_k_k, rwkv_k_a,
           rwkv_r_k, rwkv_lnx_g, rwkv_lnx_b, fox_bf, p_a, p_b, w_out, norm2_g, w_gate, w_up, w_down,
           final_g, _cfg=None):
    f = lambda a: np.ascontiguousarray(np.asarray(a, dtype=np.float32))
    x_prompt = f(x_prompt)
    L = w_in.shape[0]
    B, S = x_prompt.shape[0], x_prompt.shape[1]
    DB = x_sample.shape[0]
    PAST = cache_fox_k.shape[2]
    cfg = _cfg or Cfg(S, L, PAST)
    nc = build(cfg)
    consts, hsel = make_consts()
    pvec = np.zeros((L, 128, PV_N), np.float32)
    prow = np.zeros((L, 3, 512), np.float32)
    for l in range(L):
        pvec[l, :, PV_G1:PV_G1 + 8] = col(norm1_g[l], 8)
        pvec[l, :, PV_MU:PV_MU + 14] = col(rwkv_mu[l], 14)
        pvec[l, :, PV_W0:PV_W0 + 4] = col(rwkv_w0[l], 4)
        pvec[l, :, PV_A0:PV_A0 + 4] = col(rwkv_a0[l], 4)
        pvec[l, :, PV_KK:PV_KK + 4] = col(rwkv_k_k[l], 4)
        pvec[l, :, PV_KA:PV_KA + 4] = col(rwkv_k_a[l], 4)
        pvec[l, :, PV_RK:PV_RK + 4] = col(np.asarray(rwkv_r_k[l]).reshape(-1), 4)
        pvec[l, :, PV_G2:PV_G2 + 8] = col(norm2_g[l], 8)
        prow[l, 0] = np.asarray(rwkv_lnx_g[l])
        prow[l, 1] = np.asarray(rwkv_lnx_b[l])
        prow[l, 2, 0:NH] = np.asarray(fox_bf[l])
    shared = dict(w_in=f(w_in), w2=f(rwkv_w2), a2=f(rwkv_a2), g2=f(rwkv_g2), p_a=f(p_a), p_b=f(p_b),
                  w_out=f(w_out), w_gate=f(w_gate), w_up=f(w_up), w_down=f(w_down), pvec=pvec, prow=prow,
                  fing=col(final_g, 8), consts=consts, hsel=hsel)
    ck = f(cache_fox_k).reshape(L, DB, PAST, 512)
    cv = f(cache_fox_v).reshape(L, DB, PAST, 512)
    clf = f(cache_fox_logf)
    sr = f(state_rwkv)
    ssh = f(state_shift).reshape(L, DB, RC)
    xs = f(x_sample)
    in_maps = []
    for c in range(8):
        b = c // 2
        sl = slice(c * NS, (c + 1) * NS)
        m = dict(shared)
        m.update(x_prompt=x_prompt[b], x_sample=np.ascontiguousarray(xs[sl].reshape(NS * LS, D)),
                 cache_k=np.ascontiguousarray(ck[:, sl]), cache_v=np.ascontiguousarray(cv[:, sl]),
                 cache_lf=np.ascontiguousarray(clf[:, sl]), state_in=np.ascontiguousarray(sr[:, sl]),
                 shift_in=np.ascontiguousarray(ssh[:, sl]))
        in_maps.append(m)
    if getattr(cfg, 'trace', False):
        res = run_bass_kernel_spmd(nc, in_maps, core_ids=list(range(8)), trace=True)
        print("EXEC_TIME_NS", res.exec_time_ns)
    else:
        res = run_bass_kernel_spmd(nc, in_maps, core_ids=list(range(8)))
    R = res.results
    ev = [R[2 * b] for b in range(B)]
    y_prompt = np.stack([r["y_prompt"] for r in ev])
    y_sample = np.concatenate([r["y_sample"].reshape(NS, LS, D) for r in R])
    kp = np.stack([r["kp"] for r in ev], 1).reshape(L, B, S, NH, HD)
    vp = np.stack([r["vp"] for r in ev], 1).reshape(L, B, S, NH, HD)
    lfp = np.stack([r["lfp"] for r in ev], 1)
    sp = np.stack([r["sp"] for r in ev], 1)
    shp = np.stack([r["shp"] for r in ev], 1).reshape(L, B, 1, RC)
    kd = np.concatenate([r["kd"].reshape(L, NS, LS, NH, HD) for r in R], 1)
    vd = np.concatenate([r["vd"].reshape(L, NS, LS, NH, HD) for r in R], 1)
    lfd = np.concatenate([r["lfd"].reshape(L, NS, LS, NH) for r in R], 1)
    sd = np.concatenate([r["sd"] for r in R], 1)
    shd = np.concatenate([r["shd"] for r in R], 1).reshape(L, DB, 1, RC)
    return (y_prompt, y_sample, kp, vp, lfp, sp, shp, kd, vd, lfd, sd, shd)
```
